# Optimizing a Trainium2 kernel written in Bass

```python
import jax
import jax.numpy as jnp
from jax import lax
import numpy as np

D_MODEL = 1024
BATCH = 32
SEQ = 2048
DEPTH = 1
DEC_BATCH = 2
DEC_SEQ = 16384
PAST_LEN = 128

D_MIX = D_MODEL
GLA_HEADS = 4
GLA_V = D_MIX // 2
GLA_DV = GLA_V // GLA_HEADS
GLA_DK = GLA_DV // 2
GLA_QK = GLA_HEADS * GLA_DK
GLA_LOWRANK = 16
GLA_NORMALIZER = 16.0
LOG_GATE_MIN = -1.0
RET_HEADS = 4
RET_V = D_MIX - GLA_V
RET_DV = RET_V // RET_HEADS
RET_DK = RET_DV
RET_QK = RET_HEADS * RET_DK
CHUNK = 64
D_FF = 4 * D_MODEL
ROPE_BASE = 10000.0
EPS = 1e-6
IN_WIDTHS = (GLA_QK, GLA_QK, GLA_V, GLA_V, GLA_LOWRANK, GLA_LOWRANK, RET_QK, RET_QK, RET_V, RET_V)
D_IN = 2 * GLA_QK + 2 * GLA_V + 2 * GLA_LOWRANK + 2 * RET_QK + 2 * RET_V

kernel_name = "hybrid_gla_retention_encoder"


def rmsnorm(x, w):
    x32 = x.astype(jnp.float32)
    y = x32 * lax.rsqrt(jnp.mean(x32 * x32, axis=-1, keepdims=True) + EPS)
    return (y * w.astype(jnp.float32)).astype(x.dtype)


def head_rmsnorm(o, w):
    return o * lax.rsqrt(jnp.mean(o * o, axis=-1, keepdims=True) + EPS) * w.astype(jnp.float32)


def head_groupnorm(o, w, b):
    H, dv = o.shape[-2], o.shape[-1]
    mu = jnp.mean(o, axis=-1, keepdims=True)
    var = jnp.mean(jnp.square(o - mu), axis=-1, keepdims=True)
    y = (o - mu) * lax.rsqrt(var + EPS)
    return y * w.astype(jnp.float32).reshape(H, dv) + b.astype(jnp.float32).reshape(H, dv)


def rotary(x, pos):
    half = x.shape[-1] // 2
    inv_freq = jnp.power(ROPE_BASE, -jnp.arange(half, dtype=jnp.float32) / half)
    ang = pos[:, None] * inv_freq[None, :]
    cos = jnp.cos(ang)[None, :, None, :]
    sin = jnp.sin(ang)[None, :, None, :]
    x1, x2 = x[..., :half], x[..., half:]
    return jnp.concatenate([x1 * cos - x2 * sin, x1 * sin + x2 * cos], axis=-1)


def split_columns(proj):
    outs = []
    start = 0
    for width in IN_WIDTHS:
        outs.append(proj[..., start:start + width])
        start += width
    return outs


def chunked_gated_recurrence(q, k, v, log_a, inclusive):
    B, L, H, dk = q.shape
    dv = v.shape[-1]
    n = L // CHUNK

    def blk(t):
        return t.astype(jnp.float32).reshape(B, n, CHUNK, H, t.shape[-1])

    q, k, v, log_a = blk(q), blk(k), blk(v), blk(log_a)
    b = jnp.cumsum(log_a, axis=2)
    b_last = b[:, :, -1:]
    q_dec = q * jnp.exp(b)
    k_inv = k * jnp.exp(-b)
    k_end = k * jnp.exp(b_last - b)
    scores = jnp.einsum('bncht,bnsht->bnhcs', q_dec, k_inv)
    mask = jnp.tril(jnp.ones((CHUNK, CHUNK), dtype=bool), k=0 if inclusive else -1)
    scores = jnp.where(mask, scores, 0.0)
    o_intra = jnp.einsum('bnhcs,bnshv->bnchv', scores, v)
    chunk_kv = jnp.einsum('bncht,bnchv->bnhtv', k_end, v)
    chunk_decay = jnp.exp(b_last[:, :, 0])

    def step(state, inp):
        kv, dec = inp
        return dec[..., None] * state + kv, state

    init = jnp.zeros((B, H, dk, dv), jnp.float32)
    _, states = lax.scan(step, init, (jnp.moveaxis(chunk_kv, 1, 0), jnp.moveaxis(chunk_decay, 1, 0)))
    states = jnp.moveaxis(states, 0, 1)
    o_inter = jnp.einsum('bncht,bnhtv->bnchv', q_dec, states)
    return (o_intra + o_inter).reshape(B, L, H, dv)


def bidirectional_recurrence(q, k, v, log_a_fwd, log_a_bwd):
    flip = lambda t: jnp.flip(t, axis=1)
    fwd = chunked_gated_recurrence(q, k, v, log_a_fwd, True)
    bwd = flip(chunked_gated_recurrence(flip(q), flip(k), flip(v), flip(log_a_bwd), False))
    return fwd + bwd


def encoder_layer(x, attn_norm_w, w_in, w_alpha_fwd, b_alpha_fwd, w_alpha_bwd, b_alpha_bwd,
                  gla_norm_w, ret_norm_w, ret_norm_b, w_out, mlp_norm_w, w_ff1, w_ff2):
    B, L, _ = x.shape
    f32 = jnp.float32
    h = rmsnorm(x, attn_norm_w)
    proj = jnp.einsum('bld,de->ble', h, w_in)
    gq, gk, gv, gg, ga_f, ga_b, rq, rk, rv, rg = split_columns(proj)

    def heads(t, n_heads):
        return t.astype(f32).reshape(B, L, n_heads, -1)

    gla_q = heads(gq, GLA_HEADS) * GLA_DK ** -0.5
    gla_k = heads(gk, GLA_HEADS)
    gla_v = heads(gv, GLA_HEADS)

    def log_gate(lr, w, b):
        z = jnp.einsum('blr,rk->blk', lr.astype(f32), w.astype(f32)) + b.astype(f32)
        log_a = jnp.maximum(jax.nn.log_sigmoid(z) / GLA_NORMALIZER, LOG_GATE_MIN)
        return log_a.reshape(B, L, GLA_HEADS, GLA_DK)

    log_a_f = log_gate(ga_f, w_alpha_fwd, b_alpha_fwd)
    log_a_b = log_gate(ga_b, w_alpha_bwd, b_alpha_bwd)
    o_gla = bidirectional_recurrence(gla_q, gla_k, gla_v, log_a_f, log_a_b)
    o_gla = head_rmsnorm(o_gla, gla_norm_w).reshape(B, L, GLA_V) * jax.nn.silu(gg.astype(f32))

    pos = jnp.arange(L, dtype=f32)
    ret_q = rotary(heads(rq, RET_HEADS), pos) * RET_DK ** -0.5
    ret_k = rotary(heads(rk, RET_HEADS), pos)
    ret_v = heads(rv, RET_HEADS)
    log_gamma = jnp.log1p(-jnp.power(2.0, -5.0 - jnp.arange(RET_HEADS, dtype=f32)))
    log_decay = jnp.broadcast_to(log_gamma[None, None, :, None], (B, L, RET_HEADS, RET_DK))
    o_ret = bidirectional_recurrence(ret_q, ret_k, ret_v, log_decay, log_decay)
    o_ret = head_groupnorm(o_ret, ret_norm_w, ret_norm_b).reshape(B, L, RET_V) * jax.nn.silu(rg.astype(f32))

    mixed = jnp.concatenate([o_gla, o_ret], axis=-1).astype(x.dtype)
    x = x + jnp.einsum('ble,ed->bld', mixed, w_out)

    h = rmsnorm(x, mlp_norm_w)
    ff = jnp.square(jax.nn.relu(jnp.einsum('bld,df->blf', h, w_ff1)))
    return x + jnp.einsum('blf,fd->bld', ff, w_ff2)


def trunk(x, attn_norm_w, w_in, w_alpha_fwd, b_alpha_fwd, w_alpha_bwd, b_alpha_bwd,
          gla_norm_w, ret_norm_w, ret_norm_b, w_out, mlp_norm_w, w_ff1, w_ff2, final_norm_w):
    for l in range(DEPTH):
        x = encoder_layer(x, attn_norm_w[l], w_in[l], w_alpha_fwd[l], b_alpha_fwd[l],
                          w_alpha_bwd[l], b_alpha_bwd[l], gla_norm_w[l], ret_norm_w[l],
                          ret_norm_b[l], w_out[l], mlp_norm_w[l], w_ff1[l], w_ff2[l])
    return rmsnorm(x, final_norm_w)


def setup_inputs(seed: int = 0) -> dict:
    key = jax.random.key(seed)
    ks = jax.random.split(key, 16)
    f32 = jnp.float32
    nrm = lambda k, shape: jax.random.normal(k, shape, f32)
    return {
        'x_prompt': nrm(ks[0], (BATCH, SEQ, D_MODEL)),
        'x_sample': nrm(ks[1], (DEC_BATCH, DEC_SEQ, D_MODEL)),
        'attn_norm_w': 1.0 + 0.02 * nrm(ks[2], (DEPTH, D_MODEL)),
        'w_in': nrm(ks[3], (DEPTH, D_MODEL, D_IN)) * D_MODEL ** -0.5,
        'w_alpha_fwd': nrm(ks[4], (DEPTH, GLA_LOWRANK, GLA_QK)) * GLA_LOWRANK ** -0.5,
        'b_alpha_fwd': 0.1 * nrm(ks[5], (DEPTH, GLA_QK)),
        'w_alpha_bwd': nrm(ks[6], (DEPTH, GLA_LOWRANK, GLA_QK)) * GLA_LOWRANK ** -0.5,
        'b_alpha_bwd': 0.1 * nrm(ks[7], (DEPTH, GLA_QK)),
        'gla_norm_w': 1.0 + 0.02 * nrm(ks[8], (DEPTH, GLA_DV)),
        'ret_norm_w': 1.0 + 0.02 * nrm(ks[9], (DEPTH, RET_V)),
        'ret_norm_b': 0.02 * nrm(ks[10], (DEPTH, RET_V)),
        'w_out': nrm(ks[11], (DEPTH, D_MIX, D_MODEL)) * D_MIX ** -0.5,
        'mlp_norm_w': 1.0 + 0.02 * nrm(ks[12], (DEPTH, D_MODEL)),
        'w_ff1': nrm(ks[13], (DEPTH, D_MODEL, D_FF)) * D_MODEL ** -0.5,
        'w_ff2': nrm(ks[14], (DEPTH, D_FF, D_MODEL)) * D_FF ** -0.5,
        'final_norm_w': 1.0 + 0.02 * nrm(ks[15], (D_MODEL,)),
    }


def reference(x_prompt, x_sample, attn_norm_w, w_in, w_alpha_fwd, b_alpha_fwd, w_alpha_bwd, b_alpha_bwd,
              gla_norm_w, ret_norm_w, ret_norm_b, w_out, mlp_norm_w, w_ff1, w_ff2, final_norm_w):
    y_prompt = trunk(x_prompt, attn_norm_w, w_in, w_alpha_fwd, b_alpha_fwd, w_alpha_bwd, b_alpha_bwd,
                     gla_norm_w, ret_norm_w, ret_norm_b, w_out, mlp_norm_w, w_ff1, w_ff2, final_norm_w)
    y_sample = trunk(x_sample, attn_norm_w, w_in, w_alpha_fwd, b_alpha_fwd, w_alpha_bwd, b_alpha_bwd,
                     gla_norm_w, ret_norm_w, ret_norm_b, w_out, mlp_norm_w, w_ff1, w_ff2, final_norm_w)
    return (y_prompt, y_sample)
```

```python
import contextlib
import numpy as np
import concourse.bass as bass
import concourse.mybir as mybir
from concourse.bass_utils import run_bass_kernel_spmd

F32 = mybir.dt.float32
BF16 = mybir.dt.bfloat16
AF = mybir.ActivationFunctionType
ALU = mybir.AluOpType
AX = mybir.AxisListType

D = 1024
DIN = 3616
DFF = 4096
EPS = 1e-6
LN_QS = float(np.log(0.125))
ENG = ['pe', 'act', 'dve', 'pool', 'sp']
C_GQ, C_GK, C_GV, C_GG, C_LR, C_RQ, C_RK, C_RV, C_RG = 0, 256, 512, 1024, 1536, 1568, 2080, 2592, 3104


class Sched:
    def __init__(self, nds=16):
        self.q = {e: [] for e in ENG}
        self.cnt = {e: 0 for e in ENG}
        self.lastw = {}
        self.readers = {}
        self.waited = {e: {} for e in ENG}
        self.dman = {'sp': 0, 'pool': 0}
        self.nds = nds
        self.rec = None
        self.alias = {}
        self.m_eng = {}
        self.m_busy = {}
        self.m_ops = []
        self.m_last = {}
        self.m_w = {}
        self.m_r = {}
        self.LAT = 0.6

    def _deps(self, eng, r, w):
        d = {}

        def add(k, v):
            if eng == 'pe' and k == 'pe':
                return
            if d.get(k, 0) < v:
                d[k] = v
        for x in r:
            t = self.lastw.get(x)
            if t:
                add(*t)
        for x in w:
            t = self.lastw.get(x)
            if t:
                add(*t)
            for k, v in self.readers.get(x, {}).items():
                add(k, v)
        out = []
        for k, v in d.items():
            if self.waited[eng].get(k, 0) < v:
                self.waited[eng][k] = v
                out.append((k, v))
        return out

    def _commit(self, tok, r, w):
        k, v = tok
        for x in r:
            rd = self.readers.setdefault(x, {})
            if rd.get(k, 0) < v:
                rd[k] = v
        for x in w:
            self.lastw[x] = tok
            self.readers[x] = {}

    def _names(self, r, w):
        isp = lambda x: len(x) == 2 and x[0] == 'P' and x[1].isdigit()
        a = self.alias
        r = [a.get(x, x) for x in r]
        w = [a.get(x, x) for x in w]
        return [x for x in r if not isp(x)], w + [x for x in r if isp(x)]

    def op(self, eng, fns, r=(), w=(), cost=0.5):
        if callable(fns):
            fns = [fns]
        r, w = self._names(r, w)
        if self.rec is not None:
            self.rec.append(('op', eng, fns, r, w, cost))
        else:
            self._issue(('op', eng, fns, r, w, cost))

    def dma(self, eng, fn, r=(), w=(), cost=2.5):
        r, w = self._names(r, w)
        if self.rec is not None:
            self.rec.append(('dma', eng, fn, r, w, cost))
        else:
            self._issue(('dma', eng, fn, r, w, cost))

    def begin(self):
        self.rec = []

    def end(self):
        r, self.rec = self.rec, None
        return r

    def _start_time(self, it, why=None):
        kind, eng, f, r, w, cost = it
        t = self.m_eng.get(eng, 0.0)
        src = ('eng', self.m_last.get(eng))
        for x in list(r) + list(w):
            tw = self.m_w.get(x)
            if tw is not None:
                tt = tw[0] + (0.0 if tw[1] == eng else self.LAT)
                if tt > t:
                    t, src = tt, ('raw:' + x, tw[2])
        for x in w:
            tr = self.m_r.get(x)
            if tr is not None:
                tt = tr[0] + (0.0 if tr[1] == eng else self.LAT)
                if tt > t:
                    t, src = tt, ('war:' + x, tr[2])
        if why is not None:
            why.append(src)
        return t

    def _issue(self, it):
        kind, eng, f, r, w, cost = it
        why = []
        t0 = self._start_time(it, why)
        idx = len(self.m_ops)
        self.m_busy[eng] = self.m_busy.get(eng, 0.0) + (0.1 if kind == 'dma' else cost)
        if kind == 'dma':
            self.m_eng[eng] = t0 + 0.1
            t1 = t0 + cost
        else:
            t1 = t0 + cost
            self.m_eng[eng] = t1
        self.m_ops.append((eng, kind, cost, t0, t1, why[0], tuple(w)))
        self.m_last[eng] = idx
        for x in r:
            tr = self.m_r.get(x)
            if tr is None or tr[0] < t1:
                self.m_r[x] = (t1, eng, idx)
        for x in w:
            self.m_w[x] = (t1, eng, idx)
            self.m_r.pop(x, None)
        if kind == 'op':
            self._op(eng, f, r, w)
        else:
            self._dma(eng, f, r, w)

    def play(self, streams, mode='model'):
        if mode != 'model':
            items = []
            for s in streams:
                n = len(s)
                for k, it in enumerate(s):
                    items.append(((k + 0.5) / n, it))
            items.sort(key=lambda t: t[0])
            for _, it in items:
                self._issue(it)
            return
        last = []
        for s in streams:
            d = {}
            for k, it in enumerate(s):
                for x in list(it[3]) + list(it[4]):
                    d[x] = k
            last.append(d)
        wset = [set(x for it in s for x in it[4]) for s in streams]

        def eligible(si, it):
            rr, ww = it[3], it[4]
            for so in range(si):
                d = last[so]
                h = heads[so]
                for x in ww:
                    k = d.get(x)
                    if k is not None and h <= k:
                        return False
                for x in rr:
                    k = d.get(x)
                    if k is not None and h <= k and x in wset[so]:
                        return False
            return True

        heads = [0] * len(streams)
        while True:
            best, bt = None, None
            for si, s in enumerate(streams):
                if heads[si] < len(s):
                    it = s[heads[si]]
                    if not eligible(si, it):
                        continue
                    t = self._start_time(it)
                    if bt is None or t < bt - 1e-9:
                        best, bt = si, t
            if best is None:
                break
            self._issue(streams[best][heads[best]])
            heads[best] += 1
        assert all(heads[si] == len(s) for si, s in enumerate(streams))

    def _op(self, eng, fns, r, w):
        waits = self._deps(eng, r, w)
        self.cnt[eng] += 1
        self.q[eng].append((waits, fns, (eng, 1)))
        self._commit((eng, self.cnt[eng]), r, w)

    def _dma(self, eng, fn, r, w):
        j = self.dman[eng] % self.nds
        n = self.dman[eng] // self.nds
        self.dman[eng] += 1
        key = 'd%s%d' % (eng, j)
        waits = self._deps(eng, r, w)
        if n > 0 and self.waited[eng].get(key, 0) < 16 * n:
            self.waited[eng][key] = 16 * n
            waits.append((key, 16 * n))
        self.q[eng].append((waits, [fn], (key, 16)))
        self._commit((key, 16 * (n + 1)), r, w)

    def final_tokens(self):
        out = []
        for eng, tot in self.dman.items():
            for j in range(self.nds):
                n = (tot - j + self.nds - 1) // self.nds if tot > j else 0
                if n > 0:
                    out.append(('d%s%d' % (eng, j), 16 * n))
        return out


class NS:
    pass


HNAMES = ['x', 'gv', 'gg', 'kendg', 'qdT', 'kiT', 'Dg', 'rv', 'rg', 'kendr', 'rkT', 'sbb1']


def build_nc(NT, BT):
    assert NT % 2 == 0
    nc = bass.Bass("TRN2", target_bir_lowering=False)
    S = Sched()

    def din(name, shape):
        return nc.dram_tensor(name, list(shape), F32, kind="ExternalInput").ap()
    xin = din("x", [NT * 128, D])
    cst = din("cst", [NT, 128, 128])
    keep_d = din("keep", [128, 2 * (NT // BT)])
    w_in_d = din("w_in", [D, DIN])
    w_out_d = din("w_out", [D, D])
    w1_d = din("w_ff1", [D, DFF])
    w2_d = din("w_ff2", [DFF, D])
    wn1_d = din("wn1t", [128, 8])
    wn2_d = din("wn2t", [128, 8])
    wnf_d = din("wnf_b", [128, D])
    wa_d = din("wa", [33, 512])
    gnw_d = din("gnw_b", [128, 128])
    rnw_d = din("rnw_b", [128, 512])
    rnb_d = din("rnb_b", [128, 512])
    ident_d = din("ident", [128, 128])
    tri_d = din("tri", [4, 128, 128])
    mr_d = din("mr", [128, 512])
    qfb_d = din("qfb", [128, 1024])
    rsc_d = din("rsc", [128, 8])
    g64_d = din("g64", [128, 4])
    yout = nc.dram_tensor("y", [NT * 128, D], F32, kind="ExternalOutput").ap()
    w1s = nc.dram_tensor("w1s", [16, 128, 2048], BF16).ap()
    w2s = nc.dram_tensor("w2s", [16, 128, 2048], BF16).ap()
    sbs = nc.dram_tensor("sbs", [NT, 128, 1024], BF16).ap()
    sc_h = nc.dram_tensor("sc_h", [NT, 128, 1024], BF16).ap()
    sc_k = nc.dram_tensor("sc_k", [NT, 128, 768], BF16).ap()
    sc_v = nc.dram_tensor("sc_v", [NT, 128, 1024], BF16).ap()
    sc_e = nc.dram_tensor("sc_e", [NT, 128, 1536], BF16).ap()
    sc_la = nc.dram_tensor("sc_la", [NT, 128, 512], F32).ap()

    with contextlib.ExitStack() as es:
        def sb(name, shape, dt=F32):
            return es.enter_context(nc.sbuf_tensor(name, list(shape), dt))
        ident = sb("ident_s", [128, 128], BF16)
        tri = sb("tri_s", [128, 4, 128])
        mr = sb("mr_s", [128, 512])
        qfb = sb("qfb_s", [128, 1024])
        rsc = sb("rsc_s", [128, 8])
        g64 = sb("g64_s", [128, 4])
        keep = sb("keep_s", [128, 2 * (NT // BT)])
        wn1 = sb("wn1_s", [128, 8])
        wn2 = sb("wn2_s", [128, 8])
        wnf = sb("wnf_s", [128, D])
        wa = sb("wa_s", [33, 512])
        gnw = sb("gnw_s", [128, 128])
        rnw = sb("rnw_s", [128, 512])
        rnb = sb("rnb_s", [128, 512])
        win = sb("win_s", [128, 8, DIN], BF16)
        wout = sb("wout_s", [128, 8, D], BF16)
        ffT = sb("ffT_s", [128, 4, 512], BF16)
        x1 = sb("x1_s", [128, 4, D])
        h2T = sb("h2T_s", [128, 8, 512], BF16)
        w1b = [sb("w1b%d" % k, [128, 8, 256], BF16) for k in range(2)]
        w2b = [sb("w2b%d" % k, [128, 2, D], BF16) for k in range(2)]
        cs = sb("cs_s", [128, 128])
        hbf = sb("hbf_s", [128, D], BF16)
        hT = sb("hT_s", [128, 8, 128], BF16)
        ss = sb("ss_s", [128, 4])
        gqk = sb("gqk_s", [128, 512], BF16)
        lrT = sb("lrT_s", [33, 128])
        T = [sb("T%d" % k, [128, 512]) for k in range(4)]
        rqb = sb("rqb_s", [128, 512], BF16)
        rkb = sb("rkb_s", [128, 512], BF16)
        qfbT = sb("qfbT_s", [128, 2, 512], BF16)
        HS = []
        for k in range(2):
            h = NS()
            h.x_sb = sb("x_s%d" % k, [128, D])
            h.gv = sb("gv_s%d" % k, [128, 512], BF16)
            h.gg = sb("gg_s%d" % k, [128, 512], BF16)
            h.kendg = sb("kendg_s%d" % k, [128, 2, 256], BF16)
            h.qdT = sb("qdT_s%d" % k, [128, 512], BF16)
            h.kiT = sb("kiT_s%d" % k, [128, 512], BF16)
            h.Dg = sb("Dg_s%d" % k, [128, 8])
            h.rv = sb("rv_s%d" % k, [128, 512], BF16)
            h.rg = sb("rg_s%d" % k, [128, 512], BF16)
            h.kendr = sb("kendr_s%d" % k, [128, 2, 512], BF16)
            h.rkT = sb("rkT_s%d" % k, [128, 512], BF16)
            h.sbb1 = sb("sbb1_s%d" % k, [128, 1024], BF16)
            h.alias = {n: ('xh%d' % k if n == 'x' else n + str(k)) for n in HNAMES}
            HS.append(h)
        TB = [sb("TB%d" % k, [128, 512]) for k in range(3)]
        hTB = sb("hTB_s", [128, 8, 128], BF16)
        mixed = sb("mixed_s", [128, D], BF16)
        ATg = sb("ATg_s", [128, 512], BF16)
        ATr = sb("ATr_s", [128, 512], BF16)
        Sst = sb("S_s", [128, 1024])
        sfb0 = sb("sfb0_s", [128, 1024], BF16)
        sfb1 = sb("sfb1_s", [128, 1024], BF16)
        sbb0 = sb("sbb0_s", [128, 1024], BF16)
        st = sb("st_s", [128, 16])
        def _mk_h(k, base, gv, rv, kendg, kendr, Dg):
            h = NS()
            h.x_sb, h.gg, h.qdT, h.kiT, h.rg, h.rkT, h.sbb1 = base.x_sb, None, None, None, None, None, None
            h.gv, h.rv, h.kendg, h.kendr, h.Dg = gv, rv, kendg, kendr, Dg
            h.alias = dict(base.alias)
            h.alias.update({n: n + str(k) for n in ('gv', 'rv', 'kendg', 'kendr', 'Dg')})
            return h
        HS.append(_mk_h(2, HS[0], ATg[:, :], ATr[:, :], mixed[:, 0:512].rearrange("p (a b) -> p a b", a=2),
                        hTB[:].rearrange("p a b -> p (a b)").rearrange("p (a b) -> p a b", a=2), st[:, 0:8]))
        HS.append(_mk_h(3, HS[1], sfb1[:, 0:512], sfb1[:, 512:1024],
                        mixed[:, 512:1024].rearrange("p (a b) -> p a b", a=2),
                        TB[0][:].bitcast(BF16).rearrange("p (a b) -> p a b", a=2), st[:, 8:16]))
        P1XNAMES = ['hT', 'hbf', 'T0', 'T1', 'hTp0', 'hTp1', 'lap0', 'lap1', 'gv2', 'rv2', 'kendg2', 'kendr2', 'Dg2', 'gv3', 'rv3', 'kendg3', 'kendr3', 'Dg3',
                    'ATg', 'ATr', 'mxg', 'mxr', 'hTBg', 'hTBr', 'sfb1', 'TB0', 'st', 'st2', 'st3']
        P = [es.enter_context(nc.psum_tensor("P%d" % k, [128, 512], F32)) for k in range(8)]
        P0b = P[0][:].bitcast(BF16)
        P3b = P[3][:].bitcast(BF16)

        PNAMES = ['cs', 'hbf', 'hT', 'ss0', 'gqk', 'lrT', 'T0', 'T1', 'T2', 'T3', 'rqb', 'rkb', 'qfbT', 'P0', 'P1', 'P2']
        V0 = NS()
        V0.cs, V0.hbf, V0.hT, V0.ss, V0.gqk, V0.lrT, V0.T, V0.rqb, V0.rkb, V0.qfbT = \
            cs, hbf, hT, ss, gqk, lrT[:, :], T, rqb, rkb, qfbT
        V0.PA = [P[0], P[1], P[2]]
        V0.PAb = P0b
        V0.alias = {}
        xf = x1[:].rearrange("p a b -> p (a b)")
        V1 = NS()
        V1.T = [xf[:, 512 * k:512 * k + 512] for k in range(4)]
        V1.cs = xf[:, 2048:2176]
        V1.lrT = xf[0:33, 2176:2304]
        V1.ss = xf[:, 2304:2308]
        V1.hbf = xf[:, 2308:2820].bitcast(BF16)
        V1.hT = xf[:, 2820:3332].bitcast(BF16).rearrange("p (a b) -> p a b", a=8)
        V1.gqk = xf[:, 3332:3588].bitcast(BF16)
        V1.rkb = xf[:, 3588:3844].bitcast(BF16)
        V1.rqb = None
        V1.qfbT = None
        V1.PA = [P[3], P[4], P[5]]
        V1.PAb = P3b
        V1.alias = {n: n + 'v1' for n in PNAMES if not n.startswith('P')}
        V1.alias.update({'P0': 'P3', 'P1': 'P4', 'P2': 'P5'})
        V1NAMES = [V1.alias[n] for n in PNAMES if not n.startswith('P')]

        def mm(out, lhsT, rhs, first):
            f = lambda e: e.matmul(out, lhsT=lhsT, rhs=rhs, start=bool(first), stop=False,
                                   skip_group_check=True)
            f.cost = max(out.free_size(), 64) / 1950.0 * (4.0 if lhsT.dtype == F32 else 1.0) + 0.01
            return f

        def pe(fns, r, w):
            if callable(fns):
                fns = [fns]
            S.op('pe', fns, r=r, w=w, cost=sum(getattr(f, 'cost', 0.14) for f in fns))

        def ecost(eng, out, psum=False):
            n = out.free_size()
            if eng == 'act':
                return 0.25 + n / 1200.0
            if eng == 'dve':
                return 0.15 + n / 960.0
            return 0.25 + n * 0.0019

        def act(out, in_, func, r, w, **kw):
            S.op('act', lambda e: e.activation(out=out, in_=in_, func=func, **kw), r=r, w=w, cost=ecost('act', out))

        def tt(eng, out, in0, in1, op, r, w):
            S.op(eng, lambda e: e.tensor_tensor(out, in0, in1, op), r=r, w=w, cost=ecost(eng, out))

        def tsc(eng, out, in0, s1, s2, op0, op1, r, w):
            if s2 is None:
                S.op(eng, lambda e: e.tensor_scalar(out, in0, s1, None, op0=op0), r=r, w=w, cost=ecost(eng, out))
            else:
                S.op(eng, lambda e: e.tensor_scalar(out, in0, s1, s2, op0=op0, op1=op1), r=r, w=w, cost=ecost(eng, out))

        def stt(out, in0, scalar, in1, op0, op1, r, w):
            S.op('dve', lambda e: e.scalar_tensor_tensor(out, in0, scalar, in1, op0=op0, op1=op1), r=r, w=w, cost=ecost('dve', out))

        def cp(eng, out, in_, r, w):
            if eng == 'act':
                act(out, in_, AF.Copy, r, w)
            else:
                S.op(eng, lambda e: e.tensor_copy(out, in_), r=r, w=w, cost=ecost(eng, out))

        def ld(out, in_, w, r=(), eng='sp'):
            S.dma(eng, lambda e: e.dma_start(out=out, in_=in_), r=r, w=w, cost=2.0 + out.nbytes() / 2.5e5)

        def transposes(pb, bank, src, n, base, r):
            fns = [(lambda e, k=k: e.transpose(pb[:, (base + k) * 128:(base + k + 1) * 128],
                                                src[:, k * 128:(k + 1) * 128], ident[:]))
                   for k in range(n)]
            pe(fns, r=list(r) + ['ident'], w=['P%d' % bank])

        def inproj(PA, hT, bank, c0, c1):
            n = c1 - c0
            fns = [mm(PA[bank][:, 0:n], hT[:, kc, :], win[:, kc, c0:c1], kc == 0) for kc in range(8)]
            pe(fns, r=['hT', 'win'], w=['P%d' % bank])

        def rstd_from_ss(ss, col, n):
            c = ss[:, col:col + 1]
            rn = 'ss%d' % col
            act(c, c, AF.Ln, [rn], [rn], scale=1.0 / n, bias=EPS)
            act(c, c, AF.Exp, [rn], [rn], scale=-0.5)

        def b4(ap2d, n=4):
            return ap2d.unsqueeze(1).to_broadcast([128, n, ap2d.shape[1]])

        def v3(ap, a):
            return ap.rearrange("p (a b) -> p a b", a=a)

        ld(tri[:], tri_d.rearrange("k p c -> p k c"), ['tri'])
        ld(mr[:], mr_d, ['mr'])
        ld(qfb[:], qfb_d, ['qfb'])
        ld(rsc[:], rsc_d, ['rsc'])
        ld(g64[:], g64_d, ['g64'])
        ld(keep[:], keep_d, ['keep'])
        ld(wn1[:], wn1_d, ['wn1'])
        ld(wn2[:], wn2_d, ['wn2'])
        ld(wnf[:], wnf_d, ['wnf'])
        ld(wa[:], wa_d, ['wa'])
        ld(gnw[:], gnw_d, ['gnw'])
        ld(rnw[:], rnw_d, ['rnw'])
        ld(rnb[:], rnb_d, ['rnb'])
        ld(ident[:], ident_d, ['ident'], eng='pool')
        ld(wout[:], w_out_d.rearrange("(kc p) n -> p kc n", p=128), ['wout'], eng='pool')
        S.op('pool', lambda e: e.memset(Sst[:], 0.0), w=['S'])
        S.op('pool', lambda e: e.memset(sbb0[:], 0.0), w=['sbb0'])
        S.op('pool', lambda e: e.memset(lrT[:], 1.0), w=['lrT'])
        x1flat = x1[:].rearrange("p a b -> p (a b)")
        for kc in range(8):
            ld(x1flat[:, 0:DIN], w_in_d[kc * 128:(kc + 1) * 128, :], ['x1'])
            tsc('dve', win[:, kc, :], x1flat[:, 0:DIN], wn1[:, kc:kc + 1], None, ALU.mult, None,
                ['x1', 'wn1'], ['win'])
        w1v = w1_d.rearrange("(kc p) f -> p kc f", p=128)
        w2v = w2_d.rearrange("(fc p) d -> p fc d", p=128)
        stg = h2T[:].rearrange("p a b -> p (a b)").bitcast(F32).rearrange("p (a b) -> p a b", a=8)

        def wprep(pc):
            S.alias = {}
            ld(stg, w1v[:, :, pc * 256:(pc + 1) * 256], ['h2T'])
            for kc in range(8):
                tsc('dve' if kc % 2 else 'pool', w1b[pc % 2][:, kc, :], stg[:, kc, :], wn2[:, kc:kc + 1], None,
                    ALU.mult, None, ['h2T', 'wn2'], ['w1b%d' % (pc % 2)])
            ld(w1s[pc], w1b[pc % 2][:].rearrange("p a b -> p (a b)"), ['w1s%d' % pc], r=['w1b%d' % (pc % 2)])
            ld(w2b[pc % 2][:], w2v[:, 2 * pc:2 * pc + 2, :], ['w2b%d' % (pc % 2)], eng='pool')
            ld(w2s[pc], w2b[pc % 2][:].rearrange("p a b -> p (a b)"), ['w2s%d' % pc], r=['w2b%d' % (pc % 2)])

        hT2 = [hT, hbf[:].rearrange("p (a b) -> p a b", a=8)]
        la2 = [T[0], T[1]]

        def front_xload(i, H):
            S.alias = dict(H.alias)
            ld(H.x_sb[:], xin[i * 128:(i + 1) * 128, :], ['x'])

        def front(i, lite, H, V, xloaded=False, xnext=None):
            S.alias = dict(H.alias)
            S.alias.update(V.alias)
            par = i % 2
            if not lite:
                S.alias.update({'hT': 'hTp%d' % par, 'T0': 'lap%d' % par})
            cs, hbf, hT, ss, gqk, lrT, T, rqb, rkb, qfbT = V.cs, V.hbf, V.hT, V.ss, V.gqk, V.lrT, V.T, V.rqb, V.rkb, V.qfbT
            PA, P0b = V.PA, V.PAb
            x_sb = H.x_sb
            if not lite:
                hT = hT2[par]
            hTf = hT[:].rearrange("p a b -> p (a b)")
            la = T[0] if lite else la2[par]
            if lite:
                if not xloaded:
                    ld(x_sb[:], xin[i * 128:(i + 1) * 128, :], ['x'])
                ld(cs[:], cst[i], ['cs'])
                act(hbf[:], x_sb[:], AF.Square, ['x'], ['hbf', 'ss0'], accum_out=ss[:, 0:1])
                rstd_from_ss(ss, 0, D)
                act(hbf[:], x_sb[:], AF.Copy, ['x', 'ss0'], ['hbf'], scale=ss[:, 0:1])
                if xnext is not None:
                    ld(x_sb[:], xin[xnext * 128:(xnext + 1) * 128, :], ['x'])
                transposes(P0b, 0, hbf, 8, 0, ['hbf'])
                cp('dve', hTf, P0b[:, 0:1024], ['P0'], ['hT'])
                ld(sc_h[i], hTf, ['sc_h%d' % i], r=['hT'])
            else:
                if i == 0:
                    ld(hTf, sc_h[i], ['hT'], r=['sc_h%d' % i])
                    ld(la[:], sc_la[i], ['T0'], r=['sc_la%d' % i])
                if i + 1 < NT:
                    ld(hT2[1 - par][:].rearrange("p a b -> p (a b)"), sc_h[i + 1], ['hTp%d' % (1 - par)],
                       r=['sc_h%d' % (i + 1)])
                    ld(la2[1 - par][:], sc_la[i + 1], ['lap%d' % (1 - par)], r=['sc_la%d' % (i + 1)])
                ld(gqk[:, 256:512], sc_k[i][:, 0:256], ['gqk'], r=['sc_k%d' % i])
                ld(cs[:], cst[i], ['cs'])
                ld(rkb[:], sc_k[i][:, 256:768], ['rkb'], r=['sc_k%d' % i])
                ld(H.kendg[:].rearrange("p a b -> p (a b)"), sc_e[i][:, 0:512], ['kendg'], r=['sc_e%d' % i])
                ld(H.gv[:], sc_v[i][:, 0:512], ['gv'], r=['sc_v%d' % i])
                ld(H.kendr[:].rearrange("p a b -> p (a b)"), sc_e[i][:, 512:1536], ['kendr'], r=['sc_e%d' % i])
                ld(H.rv[:], sc_v[i][:, 512:1024], ['rv'], r=['sc_v%d' % i])
                ld(H.sbb1[:], sbs[i], ['sbb1'], r=['sbs%d' % i])
                ld(x_sb[:], xin[i * 128:(i + 1) * 128, :], ['x'])
            if lite:
                inproj(PA, hT, 1, C_GK, C_GK + 256)
                cp('act', gqk[:, 256:512], PA[1][:, 0:256], ['P1'], ['gqk'])
                ld(sc_k[i][:, 0:256], gqk[:, 256:512], ['sc_k%d' % i], r=['gqk'])
                inproj(PA, hT, 2, C_GV, C_GV + 512)
                cp('dve', H.gv[:], PA[2][:], ['P2'], ['gv'])
                ld(sc_v[i][:, 0:512], H.gv[:], ['sc_v%d' % i], r=['gv'])
            else:
                inproj(PA, hT, 1, C_GQ, C_GQ + 256)
                cp('act', gqk[:, 0:256], PA[1][:, 0:256], ['P1'], ['gqk'])
            if not lite:
                inproj(PA, hT, 1, C_GG, C_GG + 512)
                act(H.gg[:], PA[1][:], AF.Silu, ['P1'], ['gg'])
            if lite:
                fns = [mm(PA[2][0:32, 0:128], win[:, kc, C_LR:C_LR + 32], hT[:, kc, :], kc == 0) for kc in range(8)]
                pe(fns, r=['hT', 'win'], w=['P2'])
                cp('dve', lrT[0:32, :], PA[2][0:32, 0:128], ['P2'], ['lrT'])
                pe(mm(PA[1][:, :], lrT, wa[:, :], True), r=['lrT', 'wa'], w=['P1'])
                act(la[:], PA[1][:], AF.Exp, ['P1'], ['T0'], scale=-1.0)
                act(la[:], la[:], AF.Ln, ['T0'], ['T0'], bias=1.0)
                tsc('dve', la[:], la[:], -1.0 / 16.0, -1.0, ALU.mult, ALU.max, ['T0'], ['T0'])
                ld(sc_la[i], la[:], ['sc_la%d' % i], r=['T0'])
            if lite:
                pe([mm(PA[2][:, 0:256], tri[:, 2, :], la[:, 0:256], True),
                            mm(PA[2][:, 256:512], tri[:, 3, :], la[:, 256:512], False)], r=['tri', 'T0'], w=['P2'])
                ek = T[1]
                act(ek[:], PA[2][:], AF.Exp, ['P2'], ['T1'])
                tt('pool', H.kendg[:], b4(gqk[:, 256:512], 2), v3(ek[:], 2), ALU.mult, ['gqk', 'T1'], ['kendg'])
                ld(sc_e[i][:, 0:512], H.kendg[:].rearrange("p a b -> p (a b)"), ['sc_e%d' % i], r=['kendg'])
            fns = []
            for d in range(2):
                for p in range(2):
                    fns.append(mm(PA[1][:, d * 256 + p * 128:d * 256 + p * 128 + 128],
                                  la[:, d * 256 + p * 128:d * 256 + p * 128 + 128], tri[:, d, :],
                                  d == 0 and p == 0))
            pe(fns, r=['tri', 'T0'], w=['P1'])
            pf = PA[1][:, 0:256].rearrange("q (p j t) -> q p j t", p=2, j=2)[:, :, :, 63]
            pb = PA[1][:, 256:512].rearrange("q (p j t) -> q p j t", p=2, j=2)[:, :, :, 0]
            act(H.Dg[:, 0:4].rearrange("q (p j) -> q p j", p=2), pf, AF.Exp, ['P1'], ['Dg'])
            act(H.Dg[:, 4:8].rearrange("q (p j) -> q p j", p=2), pb, AF.Exp, ['P1'], ['Dg'])
            if not lite:
                act(T[2][:], PA[1][:], AF.Exp, ['P1'], ['T2'], bias=LN_QS)
                act(T[3][:], PA[1][:], AF.Exp, ['P1'], ['T3'], scale=-1.0)
                transposes(P0b, 0, gqk, 4, 0, ['gqk'])
                tt('dve', v3(H.qdT[:], 2), b4(P0b[:, 0:256], 2), v3(T[2][:], 2), ALU.mult, ['P0', 'T2'], ['qdT'])
                tt('dve', v3(H.kiT[:], 2), b4(P0b[:, 256:512], 2), v3(T[3][:], 2), ALU.mult, ['P0', 'T3'], ['kiT'])
            cosb = b4(cs[:, 0:64])
            sinb = b4(cs[:, 64:128])

            def rotary(bank, dst, dname, ta, tb):
                src = PA[bank][:].rearrange("p (h t f) -> p h t f", h=4, t=2)
                dv = dst[:].rearrange("p (h t f) -> p h t f", h=4, t=2)
                m1 = T[ta][:].rearrange("p (h t f) -> p h t f", h=4, t=2)
                m2 = T[tb][:].rearrange("p (h t f) -> p h t f", h=4, t=2)
                bk = 'P%d' % bank
                na, nb = 'T%d' % ta, 'T%d' % tb
                tt('dve', m1[:, :, 0, :], src[:, :, 0, :], cosb, ALU.mult, [bk, 'cs'], [na])
                tt('dve', m1[:, :, 1, :], src[:, :, 1, :], cosb, ALU.mult, [bk, 'cs'], [na])
                tt('dve', m2[:, :, 0, :], src[:, :, 1, :], sinb, ALU.mult, [bk, 'cs'], [nb])
                tt('dve', m2[:, :, 1, :], src[:, :, 0, :], sinb, ALU.mult, [bk, 'cs'], [nb])
                tt('pool', dv[:, :, 0, :], m1[:, :, 0, :], m2[:, :, 0, :], ALU.subtract, [na, nb], [dname])
                tt('pool', dv[:, :, 1, :], m1[:, :, 1, :], m2[:, :, 1, :], ALU.add, [na, nb], [dname])

            if lite:
                inproj(PA, hT, 2, C_RK, C_RK + 512)
                rotary(2, rkb, 'rkb', 0, 1)
                tt('pool', v3(H.kendr[:, 0, :], 4), v3(rkb[:], 4), rsc[:, 0:4].unsqueeze(2).to_broadcast([128, 4, 128]),
                   ALU.mult, ['rkb', 'rsc'], ['kendr'])
                tt('pool', v3(H.kendr[:, 1, :], 4), v3(rkb[:], 4), rsc[:, 4:8].unsqueeze(2).to_broadcast([128, 4, 128]),
                   ALU.mult, ['rkb', 'rsc'], ['kendr'])
                inproj(PA, hT, 1, C_RV, C_RV + 512)
                cp('act', H.rv[:], PA[1][:], ['P1'], ['rv'])
                ld(sc_k[i][:, 256:768], rkb[:], ['sc_k%d' % i], r=['rkb'])
                ld(sc_e[i][:, 512:1536], H.kendr[:].rearrange("p a b -> p (a b)"), ['sc_e%d' % i], r=['kendr'])
                ld(sc_v[i][:, 512:1024], H.rv[:], ['sc_v%d' % i], r=['rv'])
            if not lite:
                inproj(PA, hT, 2, C_RQ, C_RQ + 512)
                rotary(2, rqb, 'rqb', 2, 3)
                inproj(PA, hT, 1, C_RG, C_RG + 512)
                act(H.rg[:], PA[1][:], AF.Silu, ['P1'], ['rg'])
                transposes(P0b, 0, rqb, 4, 0, ['rqb'])
                transposes(P0b, 0, rkb, 4, 4, ['rkb'])
                tt('dve', qfbT[:, 0, :], P0b[:, 0:512], qfb[:, 0:512], ALU.mult, ['P0', 'qfb'], ['qfbT'])
                tt('dve', qfbT[:, 1, :], P0b[:, 0:512], qfb[:, 512:1024], ALU.mult, ['P0', 'qfb'], ['qfbT'])
                cp('act', H.rkT[:], P0b[:, 512:1024], ['P0'], ['rkT'])

        def kv_update(H, d, j, bg, br, state, sname, src=None, srcname=None):
            if src is None:
                src, srcname = state, sname
            rows = slice(64 * j, 64 * j + 64)
            fns = [mm(P[bg][:, 256 * p:256 * p + 256], H.kendg[rows, d, 128 * p:128 * p + 128],
                      H.gv[rows, 256 * p:256 * p + 256], p == 0) for p in range(2)]
            pe(fns, r=['kendg', 'gv'], w=['P%d' % bg])
            fns = [mm(P[br][:, 128 * h:128 * h + 128], H.kendr[rows, d, 128 * h:128 * h + 128],
                      H.rv[rows, 128 * h:128 * h + 128], h == 0) for h in range(4)]
            pe(fns, r=['kendr', 'rv'], w=['P%d' % br])
            for h in range(4):
                p, m = divmod(h, 2)
                rr = slice(64 * m, 64 * m + 64)
                cc = slice(256 * p + 128 * m, 256 * p + 128 * m + 128)
                dc = d * 4 + p * 2 + j
                stt(state[rr, cc], src[rr, cc], H.Dg[rr, dc:dc + 1], P[bg][rr, cc], ALU.mult, ALU.add,
                    [srcname, 'Dg', 'P%d' % bg], [sname])
            for h in range(4):
                cc = slice(512 + 128 * h, 512 + 128 * h + 128)
                stt(state[:, cc], src[:, cc], g64[:, h:h + 1], P[br][:, 128 * h:128 * h + 128],
                    ALU.mult, ALU.add, [srcname, 'g64', 'P%d' % br], [sname])

        def barrier(names):
            S.alias = {}
            S.op('pool', lambda e: e.memset(st[:, 0:1], 0.0), r=[], w=['st'] + list(names))

        def pass1_update(i, H):
            S.alias = H.alias
            if i % BT == BT - 1:
                kc = NT // BT + i // BT
                tsc('pool', Sst[:], Sst[:], keep[:, kc:kc + 1], None, ALU.mult, None, ['S', 'keep'], ['S'])
            cp('act', sfb0[:], Sst[:], ['S'], ['sfb0'])
            ld(sbs[i], sfb0[:], ['sbs%d' % i], r=['sfb0'])
            kv_update(H, 1, 1, 6, 7, Sst, 'S')
            kv_update(H, 1, 0, 6, 7, Sst, 'S')

        def pass1():
            barrier(['x1'] + V1NAMES + P1XNAMES)
            S.op('pool', lambda e: e.memset(V1.lrT, 1.0), w=['lrTv1'])
            wq = list(range(16))
            prev = None
            k = 0
            front_xload(NT - 1, HS[1])
            front_xload(NT - 2, HS[0])
            for i in reversed(range(0, NT, 2)):
                h1, h0 = (HS[1], HS[0]) if k % 2 == 0 else (HS[3], HS[2])
                k += 1
                S.begin()
                front(i + 1, True, h1, V1, xloaded=True, xnext=(i - 1 if i >= 2 else None))
                a1 = S.end()
                S.begin()
                front(i, True, h0, V0, xloaded=True, xnext=(i - 2 if i >= 2 else None))
                a0 = S.end()
                strs = [a1, a0]
                if prev is not None:
                    S.begin()
                    pass1_update(prev[0], prev[1])
                    pass1_update(prev[2], prev[3])
                    strs.insert(0, S.end())
                if wq:
                    S.begin()
                    wprep(wq.pop(0))
                    strs.append(S.end())
                S.play(strs, mode=_MODE)
                prev = (i + 1, h1, i, h0)
            pass1_update(prev[0], prev[1])
            pass1_update(prev[2], prev[3])
            while wq:
                wprep(wq.pop(0))
            barrier(['x1'] + V1NAMES + P1XNAMES)

        def stage_b(i, H):
            S.alias = H.alias
            t = i % 3
            x_sb = H.x_sb
            qdT, kiT, gv, rv = H.qdT, H.kiT, H.gv, H.rv
            if i % BT == 0:
                tsc('pool', Sst[:], Sst[:], keep[:, i // BT:i // BT + 1], None, ALU.mult, None, ['S', 'keep'], ['S'])
            tt('pool', v3(TB[1][:], 4), v3(H.gg[:], 4), b4(gnw[:]), ALU.mult, ['gg', 'gnw'], ['TB1'])
            tt('pool', TB[2][:], H.rg[:], rnw[:], ALU.mult, ['rg', 'rnw'], ['TB2'])
            tt('pool', TB[0][:], H.rg[:], rnb[:], ALU.mult, ['rg', 'rnb'], ['TB0'])
            cp('act', sfb0[:], Sst[:], ['S'], ['sfb0'])
            kv_update(H, 0, 0, 7, 3, Sst, 'S')
            cp('act', sfb1[:], Sst[:], ['S'], ['sfb1'])
            for m, bank in ((0, 4), (1, 5)):
                rr = slice(64 * m, 64 * m + 64)
                fns = []
                for d in range(2):
                    for p in range(2):
                        cc = slice(d * 256 + p * 128, d * 256 + p * 128 + 128)
                        fns.append(mm(P[bank][:, cc], kiT[rr, cc], qdT[rr, cc], d == 0 and p == 0))
                pe(fns, r=['kiT', 'qdT'], w=['P%d' % bank])
            t1 = mixed[:, 0:512]
            t2 = mixed[:, 512:1024]
            t1v = t1.rearrange("s (p m c) -> s p m c", p=2, m=2)
            t2v = t2.rearrange("s (p m c) -> s p m c", p=2, m=2)
            for m, bank in ((0, 4), (1, 5)):
                tt('dve', t1v[:, :, m, :], v3(P[bank][:, 0:256], 2), b4(tri[:, 0, :], 2), ALU.mult,
                   ['P%d' % bank, 'tri'], ['mxg'])
                tt('dve', t2v[:, :, m, :], v3(P[bank][:, 256:512], 2), b4(tri[:, 2, :], 2), ALU.mult,
                   ['P%d' % bank, 'tri'], ['mxr'])
            tt('dve', ATg[:], t1, t2, ALU.add, ['mxg', 'mxr'], ['ATg'])
            fns = [mm(P[6][:, 128 * h:128 * h + 128], H.rkT[:, 128 * h:128 * h + 128],
                      qfbT[:, 0, 128 * h:128 * h + 128], h == 0) for h in range(4)]
            pe(fns, r=['rkT', 'qfbT'], w=['P6'])
            tt('dve', ATr[:], P[6][:], mr[:], ALU.mult, ['P6', 'mr'], ['ATr'])
            kv_update(H, 1, 1, 7, 3, sbb0, 'sbb0', H.sbb1, 'sbb1')
            kv_update(H, 0, 1, 7, 3, Sst, 'S')
            fns = [mm(P[4][:, 128 * h:128 * h + 128], ATg[:, 128 * h:128 * h + 128], gv[:, 128 * h:128 * h + 128],
                      h == 0) for h in range(4)]
            for j, sfb, sbb in ((0, sfb0, sbb0), (1, sfb1, H.sbb1)):
                rows = slice(64 * j, 64 * j + 64)
                for p in range(2):
                    cc = slice(256 * p, 256 * p + 256)
                    c0 = 128 * p + 64 * j
                    fns.append(mm(P[4][rows, cc], qdT[:, c0:c0 + 64], sfb[:, cc], False))
                    fns.append(mm(P[4][rows, cc], qdT[:, 256 + c0:256 + c0 + 64], sbb[:, cc], False))
            pe(fns, r=['ATg', 'gv', 'qdT', 'sfb0', 'sfb1', 'sbb0', 'sbb1'], w=['P4'])
            fns = [mm(P[5][:, 128 * h:128 * h + 128], ATr[:, 128 * h:128 * h + 128], rv[:, 128 * h:128 * h + 128],
                      h == 0) for h in range(4)]
            for j, sfb, sbb in ((0, sfb0, sbb0), (1, sfb1, H.sbb1)):
                rows = slice(64 * j, 64 * j + 64)
                for h in range(4):
                    cc = slice(128 * h, 128 * h + 128)
                    sc = slice(512 + 128 * h, 512 + 128 * h + 128)
                    fns.append(mm(P[5][rows, cc], qfbT[:, 0, 128 * h + 64 * j:128 * h + 64 * j + 64], sfb[:, sc], False))
                    fns.append(mm(P[5][rows, cc], qfbT[:, 1, 128 * h + 64 * j:128 * h + 64 * j + 64], sbb[:, sc], False))
            pe(fns, r=['ATr', 'rv', 'qfbT', 'sfb0', 'sfb1', 'sbb0', 'sbb1'], w=['P5'])
            for h in range(4):
                cc = slice(128 * h, 128 * h + 128)
                act(hTB[:, h, :], P[4][:, cc], AF.Square, ['P4'], ['hTBg', 'st'], accum_out=st[:, h:h + 1])
            act(st[:, 0:4], st[:, 0:4], AF.Ln, ['st'], ['st'], scale=1.0 / 128, bias=EPS)
            act(st[:, 0:4], st[:, 0:4], AF.Exp, ['st'], ['st'], scale=-0.5)
            for h in range(4):
                cc = slice(128 * h, 128 * h + 128)
                stt(mixed[:, cc], P[4][:, cc], st[:, h:h + 1], TB[1][:, cc], ALU.mult, ALU.mult,
                    ['P4', 'st', 'TB1'], ['mxg'])
            S.op('dve', lambda e: e.reduce_sum(st[:, 4:8], v3(P[5][:], 4), axis=AX.X), r=['P5'], w=['st2'], cost=0.7)
            for h in range(4):
                cc = slice(128 * h, 128 * h + 128)
                act(hTB[:, 4 + h, :], P[5][:, cc], AF.Square, ['P5'], ['hTBr', 'st3'], accum_out=st[:, 8 + h:9 + h])
            tsc('dve', st[:, 4:8], st[:, 4:8], 1.0 / 128, None, ALU.mult, None, ['st2'], ['st2'])
            tt('dve', st[:, 12:16], st[:, 4:8], st[:, 4:8], ALU.mult, ['st2'], ['st2'])
            stt(st[:, 8:12], st[:, 8:12], 1.0 / 128, st[:, 12:16], ALU.mult, ALU.subtract, ['st2', 'st3'], ['st3'])
            for h in range(4):
                cc = slice(128 * h, 128 * h + 128)
                stt(TB[2][:, cc], P[5][:, cc], st[:, 4 + h:5 + h], TB[2][:, cc], ALU.subtract, ALU.mult,
                    ['P5', 'st2', 'TB2'], ['TB2'])
            act(st[:, 8:12], st[:, 8:12], AF.Ln, ['st3'], ['st3'], bias=EPS)
            act(st[:, 8:12], st[:, 8:12], AF.Exp, ['st3'], ['st3'], scale=-0.5)
            for h in range(4):
                cc = slice(128 * h, 128 * h + 128)
                stt(mixed[:, 512 + 128 * h:512 + 128 * h + 128], TB[2][:, cc], st[:, 8 + h:9 + h], TB[0][:, cc],
                    ALU.mult, ALU.add, ['TB2', 'st3', 'TB0'], ['mxr'])
            hTBf = hTB[:].rearrange("p a b -> p (a b)")
            for hf, mx, hn in ((0, 'mxg', 'hTBg'), (1, 'mxr', 'hTBr')):
                fns = [(lambda e, k=k: e.transpose(P3b[:, k * 128:(k + 1) * 128], mixed[:, k * 128:(k + 1) * 128], ident[:]))
                       for k in range(4 * hf, 4 * hf + 4)]
                pe(fns, r=[mx, 'ident'], w=['P3'])
                cp('act', hTBf[:, 512 * hf:512 * hf + 512], P3b[:, 512 * hf:512 * hf + 512], ['P3'], [hn])
                for n in range(2):
                    fns = [mm(P[6 + n][:, :], hTB[:, kc, :], wout[:, kc, 512 * n:512 * n + 512], kc == 0)
                           for kc in range(4 * hf, 4 * hf + 4)]
                    pe(fns, r=[hn, 'wout'], w=['P%d' % (6 + n)])
            for n in range(2):
                tt('dve', x1[:, t, 512 * n:512 * n + 512], P[6 + n][:, :], x_sb[:, 512 * n:512 * n + 512], ALU.add,
                   ['P%d' % (6 + n), 'x'], ['x1'])
            act(mixed[:], x1[:, t, :], AF.Square, ['x1'], ['mxg', 'mxr', 'ss1'], accum_out=ss[:, 1:2])
            rstd_from_ss(ss, 1, D)
            act(mixed[:], x1[:, t, :], AF.Copy, ['x1', 'ss1'], ['mxg', 'mxr'], scale=ss[:, 1:2])
            transposes(P3b, 3, mixed, 8, 0, ['mxg', 'mxr'])
            cp('dve', h2T[:, :, 128 * t:128 * t + 128], v3(P3b[:, 0:1024], 8), ['P3'], ['h2T'])

        def ffn(g, gs):
            S.alias = {}
            rtmp = TB[0]
            ntok = 128 * gs

            def ffn1(ft):
                pc = ft // 2
                if ft % 2 == 0:
                    ld(w1b[pc % 2][:].rearrange("p a b -> p (a b)"), w1s[pc], ['w1b%d' % (pc % 2)], r=['w1s%d' % pc])
                bank = 6 + (ft % 2)
                fns = [mm(P[bank][:, 0:ntok], w1b[pc % 2][:, kc, (ft % 2) * 128:(ft % 2) * 128 + 128],
                          h2T[:, kc, 0:ntok], kc == 0) for kc in range(8)]
                pe(fns, r=['w1b%d' % (pc % 2), 'h2T'], w=['P%d' % bank])
                act(rtmp[:, 0:ntok], P[bank][:, 0:ntok], AF.Relu, ['P%d' % bank], ['TB0'])
                tt('pool', ffT[:, ft % 4, 0:ntok], rtmp[:, 0:ntok], rtmp[:, 0:ntok], ALU.mult, ['TB0'], ['ffT%d' % (ft % 4)])

            def ffn2(ft):
                q = ft // 2
                if ft % 2 == 0:
                    ld(w2b[q % 2][:].rearrange("p a b -> p (a b)"), w2s[q], ['w2b%d' % (q % 2)], r=['w2s%d' % q])
                fns = []
                for t in range(gs):
                    for n in range(2):
                        fns.append(mm(P[2 * t + n][:, :], ffT[:, ft % 4, 128 * t:128 * t + 128],
                                      w2b[q % 2][:, ft % 2, 512 * n:512 * n + 512], ft == 0))
                pe(fns, r=['ffT%d' % (ft % 4), 'w2b%d' % (q % 2)], w=['P%d' % k for k in range(2 * gs)])

            ffn1(0)
            for ft in range(32):
                if ft + 1 < 32:
                    ffn1(ft + 1)
                ffn2(ft)
            for t in range(gs):
                for n in range(2):
                    sl = x1[:, t, 512 * n:512 * n + 512]
                    tt('dve', sl, P[2 * t + n][:, :], sl, ALU.add, ['P%d' % (2 * t + n), 'x1'], ['x1'])
            for t in range(gs):
                i = 3 * g + t
                act(mixed[:], x1[:, t, :], AF.Square, ['x1'], ['mxg', 'mxr', 'ss2'], accum_out=ss[:, 2:3])
                rstd_from_ss(ss, 2, D)
                stt(x1[:, t, :], x1[:, t, :], ss[:, 2:3], wnf[:], ALU.mult, ALU.mult, ['x1', 'ss2', 'wnf'], ['x1'])
                ld(yout[i * 128:(i + 1) * 128, :], x1[:, t, :], ['y%d' % i], r=['x1'])

        _MODE = 'model'
        S.marks = [max(S.m_eng.values())]
        pass1()
        S.marks.append(max(S.m_eng.values()))
        front(0, False, HS[0], V0)
        for i in range(NT):
            S.begin()
            stage_b(i, HS[i % 2])
            sb_ops = S.end()
            sa_ops = []
            if i + 1 < NT:
                S.begin()
                front(i + 1, False, HS[(i + 1) % 2], V0)
                sa_ops = S.end()
            S.play([s for s in (sb_ops, sa_ops) if s], mode=_MODE)
            if i % 3 == 2 or i == NT - 1:
                ffn(i // 3, i % 3 + 1)

        sems = {e: es.enter_context(nc.semaphore("sem_" + e)) for e in ENG if e != 'sp'}
        for de in ('sp', 'pool'):
            for j in range(S.nds):
                sems['d%s%d' % (de, j)] = es.enter_context(nc.semaphore("dsem_%s%d" % (de, j)))
        fin = S.final_tokens()
        block = es.enter_context(nc.Block())

        def emit(name, e):
            for waits, fns, (k, inc) in S.q[name]:
                for wk, wv in waits:
                    e.wait_ge(sems[wk], wv)
                for f in fns[:-1]:
                    f(e)
                fns[-1](e).then_inc(sems[k], inc)

        @block.tensor
        def _(e):
            emit('pe', e)

        @block.scalar
        def _(e):
            emit('act', e)

        @block.vector
        def _(e):
            emit('dve', e)

        @block.gpsimd
        def _(e):
            emit('pool', e)

        @block.sync
        def _(e):
            emit('sp', e)
            for k, v in fin:
                e.wait_ge(sems[k], v)
    return nc


RET_HEADS = 4
ROPE_BASE = 10000.0


def _const_tables():
    s = np.arange(128)[:, None]
    c = np.arange(128)[None, :]
    same = (s // 64) == (c // 64)
    tri = np.stack([same & (s <= c), same & (s >= c), same & (s > c), same & (s < c)]).astype(np.float32)
    lg = np.log1p(-np.power(2.0, -5.0 - np.arange(RET_HEADS, dtype=np.float64)))
    cc = (np.arange(128) % 64).astype(np.float64)
    mr = np.zeros((128, 4, 128), np.float64)
    qf = np.zeros((4, 128), np.float64)
    qb = np.zeros((4, 128), np.float64)
    rsc = np.zeros((128, 8), np.float64)
    for h in range(4):
        e = np.abs(c - s) - (cc[None, :] + 1.0)
        mr[:, h, :] = np.where(same, np.exp(lg[h] * e), 0.0)
        qf[h] = np.exp(lg[h] * (cc + 1.0)) * 128.0 ** -0.5
        qb[h] = np.exp(lg[h] * (64.0 - cc)) * 128.0 ** -0.5
        rsc[:, h] = np.exp(lg[h] * (63.0 - cc))
        rsc[:, 4 + h] = np.exp(lg[h] * cc)
    qfb = np.concatenate([np.broadcast_to(qf.reshape(1, 512), (128, 512)),
                          np.broadcast_to(qb.reshape(1, 512), (128, 512))], axis=1)
    g64 = np.broadcast_to(np.exp(lg * 64.0)[None, :], (128, 4))
    return dict(tri=tri, mr=mr.reshape(128, 512).astype(np.float32), qfb=np.ascontiguousarray(qfb, np.float32),
                rsc=rsc.astype(np.float32), g64=np.ascontiguousarray(g64, np.float32),
                ident=np.eye(128, dtype=np.float32))


def _rope_table(L):
    half = 64
    inv_freq = np.power(np.float32(ROPE_BASE), -np.arange(half, dtype=np.float32) / np.float32(half)).astype(np.float32)
    ang = np.arange(L, dtype=np.float32)[:, None] * inv_freq[None, :]
    return np.concatenate([np.cos(ang), np.sin(ang)], axis=1).astype(np.float32)


def _shared_inputs(inp):
    f = lambda a: np.ascontiguousarray(np.asarray(a, dtype=np.float32))
    wa = np.zeros((33, 512), np.float32)
    wa[0:16, 0:256] = f(inp['w_alpha_fwd'])[0]
    wa[16:32, 256:512] = f(inp['w_alpha_bwd'])[0]
    wa[32, 0:256] = f(inp['b_alpha_fwd'])[0]
    wa[32, 256:512] = f(inp['b_alpha_bwd'])[0]
    d = dict(
        w_in=f(inp['w_in'])[0], w_out=f(inp['w_out'])[0], w_ff1=f(inp['w_ff1'])[0], w_ff2=f(inp['w_ff2'])[0],
        wn1t=np.ascontiguousarray(f(inp['attn_norm_w'])[0].reshape(8, 128).T),
        wn2t=np.ascontiguousarray(f(inp['mlp_norm_w'])[0].reshape(8, 128).T),
        wnf_b=np.ascontiguousarray(np.broadcast_to(f(inp['final_norm_w'])[None, :], (128, D))),
        wa=wa,
        gnw_b=np.ascontiguousarray(np.broadcast_to(f(inp['gla_norm_w'])[0][None, :], (128, 128))),
        rnw_b=np.ascontiguousarray(np.broadcast_to(f(inp['ret_norm_w'])[0][None, :], (128, 512))),
        rnb_b=np.ascontiguousarray(np.broadcast_to(f(inp['ret_norm_b'])[0][None, :], (128, 512))),
    )
    d.update(_const_tables())
    return d


def _core_inputs(seqs, NT, shared, BT=16):
    x = np.zeros((NT * 128, D), np.float32)
    cst = np.zeros((NT, 128, 128), np.float32)
    NB = NT // BT
    keep = np.ones((128, 2 * NB), np.float32)
    pos = 0
    for s in seqs:
        L = s.shape[0]
        x[pos:pos + L] = s
        cst[pos // 128:(pos + L) // 128] = _rope_table(L).reshape(L // 128, 128, 128)
        assert (pos // 128) % BT == 0 and ((pos + L) // 128) % BT == 0
        keep[:, (pos // 128) // BT] = 0.0
        keep[:, NB + ((pos + L) // 128 - 1) // BT] = 0.0
        pos += L
    if pos < NT * 128:
        keep[:, (pos // 128) // BT] = 0.0
        keep[:, NB + NB - 1] = 0.0
    m = dict(shared)
    m.update(x=x, cst=cst, keep=keep)
    return m


_NC_CACHE = {}


def kernel(x_prompt, x_sample, **w):
    xp = np.asarray(x_prompt, dtype=np.float32)
    xs = np.asarray(x_sample, dtype=np.float32)
    NT, BT = 128, 16
    shared = _shared_inputs(w)
    plan = [[('s', 0)], [('s', 1)]]
    counts = [6, 6, 5, 5, 5, 5]
    b = 0
    for c in counts:
        plan.append([('p', b + k) for k in range(c)])
        b += c
    in_maps = []
    for core in plan:
        seqs = [xs[k] if kind == 's' else xp[k] for kind, k in core]
        in_maps.append(_core_inputs(seqs, NT, shared, BT))
    key = (NT, BT)
    if key not in _NC_CACHE:
        _NC_CACHE[key] = build_nc(NT, BT)
    res = run_bass_kernel_spmd(_NC_CACHE[key], in_maps, core_ids=list(range(8)))
    yp = np.zeros_like(xp)
    ys = np.zeros_like(xs)
    for ci, core in enumerate(plan):
        y = res.results[ci]["y"]
        pos = 0
        for kind, k in core:
            if kind == 's':
                ys[k] = y[pos:pos + xs.shape[1]]
                pos += xs.shape[1]
            else:
                yp[k] = y[pos:pos + xp.shape[1]]
                pos += xp.shape[1]
    return yp, ys
```

```python
import contextlib
import numpy as np
import concourse.bass as bass
import concourse.mybir as mybir
from concourse.bass_utils import run_bass_kernel_spmd

F32 = mybir.dt.float32
BF16 = mybir.dt.bfloat16
AF = mybir.ActivationFunctionType
ALU = mybir.AluOpType
AX = mybir.AxisListType

D = 1024
DIN = 3616
DFF = 4096
EPS = 1e-6
LN_QS = float(np.log(0.125))
ENG = ['pe', 'act', 'dve', 'pool', 'sp']
C_GQ, C_GK, C_GV, C_GG, C_LR, C_RQ, C_RK, C_RV, C_RG = 0, 256, 512, 1024, 1536, 1568, 2080, 2592, 3104


class Sched:
    def __init__(self, nds=16):
        self.q = {e: [] for e in ENG}
        self.cnt = {e: 0 for e in ENG}
        self.lastw = {}
        self.readers = {}
        self.waited = {e: {} for e in ENG}
        self.dman = {'sp': 0, 'pool': 0}
        self.nds = nds
        self.rec = None
        self.alias = {}
        self.m_eng = {}
        self.m_busy = {}
        self.m_ops = []
        self.m_last = {}
        self.m_w = {}
        self.m_r = {}
        self.LAT = 1.0

    def _deps(self, eng, r, w):
        d = {}

        def add(k, v):
            if eng == 'pe' and k == 'pe':
                return
            if d.get(k, 0) < v:
                d[k] = v
        for x in r:
            t = self.lastw.get(x)
            if t:
                add(*t)
        for x in w:
            t = self.lastw.get(x)
            if t:
                add(*t)
            for k, v in self.readers.get(x, {}).items():
                add(k, v)
        out = []
        for k, v in d.items():
            if self.waited[eng].get(k, 0) < v:
                self.waited[eng][k] = v
                out.append((k, v))
        return out

    def _commit(self, tok, r, w):
        k, v = tok
        for x in r:
            rd = self.readers.setdefault(x, {})
            if rd.get(k, 0) < v:
                rd[k] = v
        for x in w:
            self.lastw[x] = tok
            self.readers[x] = {}

    def _names(self, r, w):
        isp = lambda x: len(x) == 2 and x[0] == 'P' and x[1].isdigit()
        a = self.alias
        r = [a.get(x, x) for x in r]
        w = [a.get(x, x) for x in w]
        return [x for x in r if not isp(x)], w + [x for x in r if isp(x)]

    def op(self, eng, fns, r=(), w=(), cost=0.5):
        if callable(fns):
            fns = [fns]
        r, w = self._names(r, w)
        if self.rec is not None:
            self.rec.append(('op', eng, fns, r, w, cost))
        else:
            self._issue(('op', eng, fns, r, w, cost))

    def dma(self, eng, fn, r=(), w=(), cost=2.5):
        r, w = self._names(r, w)
        if self.rec is not None:
            self.rec.append(('dma', eng, fn, r, w, cost))
        else:
            self._issue(('dma', eng, fn, r, w, cost))

    def begin(self):
        self.rec = []

    def end(self):
        r, self.rec = self.rec, None
        return r

    def _start_time(self, it, why=None):
        kind, eng, f, r, w, cost = it
        t = self.m_eng.get(eng, 0.0)
        src = ('eng', self.m_last.get(eng))
        for x in list(r) + list(w):
            tw = self.m_w.get(x)
            if tw is not None:
                tt = tw[0] + (0.0 if tw[1] == eng else self.LAT)
                if tt > t:
                    t, src = tt, ('raw:' + x, tw[2])
        for x in w:
            tr = self.m_r.get(x)
            if tr is not None:
                tt = tr[0] + (0.0 if tr[1] == eng else self.LAT)
                if tt > t:
                    t, src = tt, ('war:' + x, tr[2])
        if why is not None:
            why.append(src)
        return t

    def _issue(self, it):
        kind, eng, f, r, w, cost = it
        why = []
        t0 = self._start_time(it, why)
        idx = len(self.m_ops)
        self.m_busy[eng] = self.m_busy.get(eng, 0.0) + (0.1 if kind == 'dma' else cost)
        if kind == 'dma':
            self.m_eng[eng] = t0 + 0.1
            t1 = t0 + cost
        else:
            t1 = t0 + cost
            self.m_eng[eng] = t1
        self.m_ops.append((eng, kind, cost, t0, t1, why[0], tuple(w)))
        self.m_last[eng] = idx
        for x in r:
            tr = self.m_r.get(x)
            if tr is None or tr[0] < t1:
                self.m_r[x] = (t1, eng, idx)
        for x in w:
            self.m_w[x] = (t1, eng, idx)
            self.m_r.pop(x, None)
        if kind == 'op':
            self._op(eng, f, r, w)
        else:
            self._dma(eng, f, r, w)

    def play(self, streams, mode='model'):
        if mode != 'model':
            items = []
            for s in streams:
                n = len(s)
                for k, it in enumerate(s):
                    items.append(((k + 0.5) / n, it))
            items.sort(key=lambda t: t[0])
            for _, it in items:
                self._issue(it)
            return
        last = []
        for s in streams:
            d = {}
            for k, it in enumerate(s):
                for x in list(it[3]) + list(it[4]):
                    d[x] = k
            last.append(d)
        wset = [set(x for it in s for x in it[4]) for s in streams]

        def eligible(si, it):
            rr, ww = it[3], it[4]
            for so in range(si):
                d = last[so]
                h = heads[so]
                for x in ww:
                    k = d.get(x)
                    if k is not None and h <= k:
                        return False
                for x in rr:
                    k = d.get(x)
                    if k is not None and h <= k and x in wset[so]:
                        return False
            return True

        heads = [0] * len(streams)
        while True:
            best, bt = None, None
            for si, s in enumerate(streams):
                if heads[si] < len(s):
                    it = s[heads[si]]
                    if not eligible(si, it):
                        continue
                    t = self._start_time(it)
                    if bt is None or t < bt - 1e-9:
                        best, bt = si, t
            if best is None:
                break
            self._issue(streams[best][heads[best]])
            heads[best] += 1
        assert all(heads[si] == len(s) for si, s in enumerate(streams))

    def _op(self, eng, fns, r, w):
        waits = self._deps(eng, r, w)
        self.cnt[eng] += 1
        self.q[eng].append((waits, fns, (eng, 1)))
        self._commit((eng, self.cnt[eng]), r, w)

    def _dma(self, eng, fn, r, w):
        j = self.dman[eng] % self.nds
        n = self.dman[eng] // self.nds
        self.dman[eng] += 1
        key = 'd%s%d' % (eng, j)
        waits = self._deps(eng, r, w)
        if n > 0 and self.waited[eng].get(key, 0) < 16 * n:
            self.waited[eng][key] = 16 * n
            waits.append((key, 16 * n))
        self.q[eng].append((waits, [fn], (key, 16)))
        self._commit((key, 16 * (n + 1)), r, w)

    def final_tokens(self):
        out = []
        for eng, tot in self.dman.items():
            for j in range(self.nds):
                n = (tot - j + self.nds - 1) // self.nds if tot > j else 0
                if n > 0:
                    out.append(('d%s%d' % (eng, j), 16 * n))
        return out


class NS:
    pass


HNAMES = ['x', 'gv', 'gg', 'kendg', 'qdT', 'kiT', 'Dg', 'rv', 'rg', 'kendr', 'rkT', 'sbb1']


def build_nc(NT, BT):
    assert NT % 2 == 0
    nc = bass.Bass("TRN2", target_bir_lowering=False)
    S = Sched()

    def din(name, shape):
        return nc.dram_tensor(name, list(shape), F32, kind="ExternalInput").ap()
    xin = din("x", [NT * 128, D])
    cst = din("cst", [NT, 128, 128])
    keep_d = din("keep", [128, 2 * (NT // BT)])
    w_in_d = din("w_in", [D, DIN])
    w_out_d = din("w_out", [D, D])
    w1_d = din("w_ff1", [D, DFF])
    w2_d = din("w_ff2", [DFF, D])
    wn1_d = din("wn1t", [128, 8])
    wn2_d = din("wn2t", [128, 8])
    wnf_d = din("wnf_b", [128, D])
    wa_d = din("wa", [33, 512])
    gnw_d = din("gnw_b", [128, 128])
    rnw_d = din("rnw_b", [128, 512])
    rnb_d = din("rnb_b", [128, 512])
    ident_d = din("ident", [128, 128])
    tri_d = din("tri", [4, 128, 128])
    mr_d = din("mr", [128, 512])
    qfb_d = din("qfb", [128, 1024])
    rsc_d = din("rsc", [128, 8])
    g64_d = din("g64", [128, 4])
    yout = nc.dram_tensor("y", [NT * 128, D], F32, kind="ExternalOutput").ap()
    w1s = nc.dram_tensor("w1s", [16, 128, 2048], BF16).ap()
    w2s = nc.dram_tensor("w2s", [16, 128, 2048], BF16).ap()
    sbs = nc.dram_tensor("sbs", [NT, 128, 1024], BF16).ap()
    sc_h = nc.dram_tensor("sc_h", [NT, 128, 1024], BF16).ap()
    sc_k = nc.dram_tensor("sc_k", [NT, 128, 768], BF16).ap()
    sc_v = nc.dram_tensor("sc_v", [NT, 128, 1024], BF16).ap()
    sc_e = nc.dram_tensor("sc_e", [NT, 128, 1536], BF16).ap()
    sc_la = nc.dram_tensor("sc_la", [NT, 128, 512], F32).ap()

    with contextlib.ExitStack() as es:
        def sb(name, shape, dt=F32):
            return es.enter_context(nc.sbuf_tensor(name, list(shape), dt))
        ident = sb("ident_s", [128, 128], BF16)
        tri = sb("tri_s", [128, 4, 128])
        mr = sb("mr_s", [128, 512])
        qfb = sb("qfb_s", [128, 1024])
        rsc = sb("rsc_s", [128, 8])
        g64 = sb("g64_s", [128, 4])
        keep = sb("keep_s", [128, 2 * (NT // BT)])
        wn1 = sb("wn1_s", [128, 8])
        wn2 = sb("wn2_s", [128, 8])
        wnf = sb("wnf_s", [128, D])
        wa = sb("wa_s", [33, 512])
        gnw = sb("gnw_s", [128, 128])
        rnw = sb("rnw_s", [128, 512])
        rnb = sb("rnb_s", [128, 512])
        win = sb("win_s", [128, 8, DIN], BF16)
        wout = sb("wout_s", [128, 8, D], BF16)
        ffT = sb("ffT_s", [128, 4, 512], BF16)
        x1 = sb("x1_s", [128, 4, D])
        h2T = sb("h2T_s", [128, 8, 512], BF16)
        w1b = [sb("w1b%d" % k, [128, 8, 256], BF16) for k in range(2)]
        w2b = [sb("w2b%d" % k, [128, 2, D], BF16) for k in range(2)]
        cs = sb("cs_s", [128, 128])
        hbf = sb("hbf_s", [128, D], BF16)
        hT = sb("hT_s", [128, 8, 128], BF16)
        ss = sb("ss_s", [128, 4])
        gqk = sb("gqk_s", [128, 512], BF16)
        lrT = sb("lrT_s", [33, 128])
        T = [sb("T%d" % k, [128, 512]) for k in range(4)]
        rqb = sb("rqb_s", [128, 512], BF16)
        rkb = sb("rkb_s", [128, 512], BF16)
        qfbT = sb("qfbT_s", [128, 2, 512], BF16)
        HS = []
        for k in range(2):
            h = NS()
            h.x_sb = sb("x_s%d" % k, [128, D])
            h.gv = sb("gv_s%d" % k, [128, 512], BF16)
            h.gg = sb("gg_s%d" % k, [128, 512], BF16)
            h.kendg = sb("kendg_s%d" % k, [128, 2, 256], BF16)
            h.qdT = sb("qdT_s%d" % k, [128, 512], BF16)
            h.kiT = sb("kiT_s%d" % k, [128, 512], BF16)
            h.Dg = sb("Dg_s%d" % k, [128, 8])
            h.rv = sb("rv_s%d" % k, [128, 512], BF16)
            h.rg = sb("rg_s%d" % k, [128, 512], BF16)
            h.kendr = sb("kendr_s%d" % k, [128, 2, 512], BF16)
            h.rkT = sb("rkT_s%d" % k, [128, 512], BF16)
            h.sbb1 = sb("sbb1_s%d" % k, [128, 1024], BF16)
            h.alias = {n: ('xh%d' % k if n == 'x' else n + str(k)) for n in HNAMES}
            HS.append(h)
        TB = [sb("TB%d" % k, [128, 512]) for k in range(3)]
        hTB = sb("hTB_s", [128, 8, 128], BF16)
        mixed = sb("mixed_s", [128, D], BF16)
        ATg = sb("ATg_s", [128, 512], BF16)
        ATr = sb("ATr_s", [128, 512], BF16)
        Sst = sb("S_s", [128, 1024])
        sfb0 = sb("sfb0_s", [128, 1024], BF16)
        sfb1 = sb("sfb1_s", [128, 1024], BF16)
        sbb0 = sb("sbb0_s", [128, 1024], BF16)
        st = sb("st_s", [128, 16])
        def _mk_h(k, base, gv, rv, kendg, kendr, Dg):
            h = NS()
            h.x_sb, h.gg, h.qdT, h.kiT, h.rg, h.rkT, h.sbb1 = base.x_sb, None, None, None, None, None, None
            h.gv, h.rv, h.kendg, h.kendr, h.Dg = gv, rv, kendg, kendr, Dg
            h.alias = dict(base.alias)
            h.alias.update({n: n + str(k) for n in ('gv', 'rv', 'kendg', 'kendr', 'Dg')})
            return h
        HS.append(_mk_h(2, HS[0], ATg[:, :], ATr[:, :], mixed[:, 0:512].rearrange("p (a b) -> p a b", a=2),
                        hTB[:].rearrange("p a b -> p (a b)").rearrange("p (a b) -> p a b", a=2), st[:, 0:8]))
        HS.append(_mk_h(3, HS[1], sfb1[:, 0:512], sfb1[:, 512:1024],
                        mixed[:, 512:1024].rearrange("p (a b) -> p a b", a=2),
                        TB[0][:].bitcast(BF16).rearrange("p (a b) -> p a b", a=2), st[:, 8:16]))
        P1XNAMES = ['hT', 'hbf', 'T0', 'T1', 'hTp0', 'hTp1', 'lap0', 'lap1', 'gv2', 'rv2', 'kendg2', 'kendr2', 'Dg2', 'gv3', 'rv3', 'kendg3', 'kendr3', 'Dg3',
                    'ATg', 'ATr', 'mxg', 'mxr', 'hTBg', 'hTBr', 'sfb1', 'TB0', 'st', 'st2', 'st3']
        P = [es.enter_context(nc.psum_tensor("P%d" % k, [128, 512], F32)) for k in range(8)]
        P0b = P[0][:].bitcast(BF16)
        P3b = P[3][:].bitcast(BF16)

        PNAMES = ['cs', 'hbf', 'hT', 'ss0', 'gqk', 'lrT', 'T0', 'T1', 'T2', 'T3', 'rqb', 'rkb', 'qfbT', 'P0', 'P1', 'P2']
        V0 = NS()
        V0.cs, V0.hbf, V0.hT, V0.ss, V0.gqk, V0.lrT, V0.T, V0.rqb, V0.rkb, V0.qfbT = \
            cs, hbf, hT, ss, gqk, lrT[:, :], T, rqb, rkb, qfbT
        V0.PA = [P[0], P[1], P[2]]
        V0.PAb = P0b
        V0.alias = {}
        xf = x1[:].rearrange("p a b -> p (a b)")
        V1 = NS()
        V1.T = [xf[:, 512 * k:512 * k + 512] for k in range(4)]
        V1.cs = xf[:, 2048:2176]
        V1.lrT = xf[0:33, 2176:2304]
        V1.ss = xf[:, 2304:2308]
        V1.hbf = xf[:, 2308:2820].bitcast(BF16)
        V1.hT = xf[:, 2820:3332].bitcast(BF16).rearrange("p (a b) -> p a b", a=8)
        V1.gqk = xf[:, 3332:3588].bitcast(BF16)
        V1.rkb = xf[:, 3588:3844].bitcast(BF16)
        V1.rqb = None
        V1.qfbT = None
        V1.PA = [P[3], P[4], P[5]]
        V1.PAb = P3b
        V1.alias = {n: n + 'v1' for n in PNAMES if not n.startswith('P')}
        V1.alias.update({'P0': 'P3', 'P1': 'P4', 'P2': 'P5'})
        V1NAMES = [V1.alias[n] for n in PNAMES if not n.startswith('P')]

        def mm(out, lhsT, rhs, first):
            f = lambda e: e.matmul(out, lhsT=lhsT, rhs=rhs, start=bool(first), stop=False,
                                   skip_group_check=True)
            f.cost = max(out.free_size(), 64) / 1950.0 * (4.0 if lhsT.dtype == F32 else 1.0) + 0.01
            return f

        def pe(fns, r, w):
            if callable(fns):
                fns = [fns]
            S.op('pe', fns, r=r, w=w, cost=sum(getattr(f, 'cost', 0.14) for f in fns))

        def ecost(eng, out, psum=False):
            n = out.free_size()
            if eng == 'act':
                return 0.25 + n / 1200.0
            if eng == 'dve':
                return 0.15 + n / 960.0
            return 0.25 + n * 0.0019

        def act(out, in_, func, r, w, **kw):
            S.op('act', lambda e: e.activation(out=out, in_=in_, func=func, **kw), r=r, w=w, cost=ecost('act', out))

        def tt(eng, out, in0, in1, op, r, w):
            S.op(eng, lambda e: e.tensor_tensor(out, in0, in1, op), r=r, w=w, cost=ecost(eng, out))

        def tsc(eng, out, in0, s1, s2, op0, op1, r, w):
            if s2 is None:
                S.op(eng, lambda e: e.tensor_scalar(out, in0, s1, None, op0=op0), r=r, w=w, cost=ecost(eng, out))
            else:
                S.op(eng, lambda e: e.tensor_scalar(out, in0, s1, s2, op0=op0, op1=op1), r=r, w=w, cost=ecost(eng, out))

        def stt(out, in0, scalar, in1, op0, op1, r, w):
            S.op('dve', lambda e: e.scalar_tensor_tensor(out, in0, scalar, in1, op0=op0, op1=op1), r=r, w=w, cost=ecost('dve', out))

        def cp(eng, out, in_, r, w):
            if eng == 'act':
                act(out, in_, AF.Copy, r, w)
            else:
                S.op(eng, lambda e: e.tensor_copy(out, in_), r=r, w=w, cost=ecost(eng, out))

        def ld(out, in_, w, r=(), eng='sp'):
            S.dma(eng, lambda e: e.dma_start(out=out, in_=in_), r=r, w=w, cost=2.0 + out.nbytes() / 2.5e5)

        def transposes(pb, bank, src, n, base, r):
            fns = [(lambda e, k=k: e.transpose(pb[:, (base + k) * 128:(base + k + 1) * 128],
                                                src[:, k * 128:(k + 1) * 128], ident[:]))
                   for k in range(n)]
            pe(fns, r=list(r) + ['ident'], w=['P%d' % bank])

        def inproj(PA, hT, bank, c0, c1):
            n = c1 - c0
            fns = [mm(PA[bank][:, 0:n], hT[:, kc, :], win[:, kc, c0:c1], kc == 0) for kc in range(8)]
            pe(fns, r=['hT', 'win'], w=['P%d' % bank])

        def rstd_from_ss(ss, col, n):
            c = ss[:, col:col + 1]
            rn = 'ss%d' % col
            act(c, c, AF.Ln, [rn], [rn], scale=1.0 / n, bias=EPS)
            act(c, c, AF.Exp, [rn], [rn], scale=-0.5)

        def b4(ap2d, n=4):
            return ap2d.unsqueeze(1).to_broadcast([128, n, ap2d.shape[1]])

        def v3(ap, a):
            return ap.rearrange("p (a b) -> p a b", a=a)

        ld(tri[:], tri_d.rearrange("k p c -> p k c"), ['tri'])
        ld(mr[:], mr_d, ['mr'])
        ld(qfb[:], qfb_d, ['qfb'])
        ld(rsc[:], rsc_d, ['rsc'])
        ld(g64[:], g64_d, ['g64'])
        ld(keep[:], keep_d, ['keep'])
        ld(wn1[:], wn1_d, ['wn1'])
        ld(wn2[:], wn2_d, ['wn2'])
        ld(wnf[:], wnf_d, ['wnf'])
        ld(wa[:], wa_d, ['wa'])
        ld(gnw[:], gnw_d, ['gnw'])
        ld(rnw[:], rnw_d, ['rnw'])
        ld(rnb[:], rnb_d, ['rnb'])
        ld(ident[:], ident_d, ['ident'], eng='pool')
        ld(wout[:], w_out_d.rearrange("(kc p) n -> p kc n", p=128), ['wout'], eng='pool')
        S.op('pool', lambda e: e.memset(Sst[:], 0.0), w=['S'])
        S.op('pool', lambda e: e.memset(sbb0[:], 0.0), w=['sbb0'])
        S.op('pool', lambda e: e.memset(lrT[:], 1.0), w=['lrT'])
        x1flat = x1[:].rearrange("p a b -> p (a b)")
        for kc in range(8):
            ld(x1flat[:, 0:DIN], w_in_d[kc * 128:(kc + 1) * 128, :], ['x1'])
            tsc('dve', win[:, kc, :], x1flat[:, 0:DIN], wn1[:, kc:kc + 1], None, ALU.mult, None,
                ['x1', 'wn1'], ['win'])
        w1v = w1_d.rearrange("(kc p) f -> p kc f", p=128)
        w2v = w2_d.rearrange("(fc p) d -> p fc d", p=128)
        stg = h2T[:].rearrange("p a b -> p (a b)").bitcast(F32).rearrange("p (a b) -> p a b", a=8)

        def wprep(pc):
            S.alias = {}
            ld(stg, w1v[:, :, pc * 256:(pc + 1) * 256], ['h2T'])
            for kc in range(8):
                tsc('dve' if kc % 2 else 'pool', w1b[pc % 2][:, kc, :], stg[:, kc, :], wn2[:, kc:kc + 1], None,
                    ALU.mult, None, ['h2T', 'wn2'], ['w1b%d' % (pc % 2)])
            ld(w1s[pc], w1b[pc % 2][:].rearrange("p a b -> p (a b)"), ['w1s%d' % pc], r=['w1b%d' % (pc % 2)])
            ld(w2b[pc % 2][:], w2v[:, 2 * pc:2 * pc + 2, :], ['w2b%d' % (pc % 2)], eng='pool')
            ld(w2s[pc], w2b[pc % 2][:].rearrange("p a b -> p (a b)"), ['w2s%d' % pc], r=['w2b%d' % (pc % 2)])

        hT2 = [hT, hbf[:].rearrange("p (a b) -> p a b", a=8)]
        la2 = [T[0], T[1]]

        def front_xload(i, H):
            S.alias = dict(H.alias)
            ld(H.x_sb[:], xin[i * 128:(i + 1) * 128, :], ['x'])

        def front(i, lite, H, V, xloaded=False, xnext=None):
            S.alias = dict(H.alias)
            S.alias.update(V.alias)
            par = i % 2
            if not lite:
                S.alias.update({'hT': 'hTp%d' % par, 'T0': 'lap%d' % par})
            cs, hbf, hT, ss, gqk, lrT, T, rqb, rkb, qfbT = V.cs, V.hbf, V.hT, V.ss, V.gqk, V.lrT, V.T, V.rqb, V.rkb, V.qfbT
            PA, P0b = V.PA, V.PAb
            x_sb = H.x_sb
            if not lite:
                hT = hT2[par]
            hTf = hT[:].rearrange("p a b -> p (a b)")
            la = T[0] if lite else la2[par]
            if lite:
                if not xloaded:
                    ld(x_sb[:], xin[i * 128:(i + 1) * 128, :], ['x'])
                ld(cs[:], cst[i], ['cs'])
                act(hbf[:], x_sb[:], AF.Square, ['x'], ['hbf', 'ss0'], accum_out=ss[:, 0:1])
                rstd_from_ss(ss, 0, D)
                act(hbf[:], x_sb[:], AF.Copy, ['x', 'ss0'], ['hbf'], scale=ss[:, 0:1])
                if xnext is not None:
                    ld(x_sb[:], xin[xnext * 128:(xnext + 1) * 128, :], ['x'])
                transposes(P0b, 0, hbf, 8, 0, ['hbf'])
                cp('dve', hTf, P0b[:, 0:1024], ['P0'], ['hT'])
                ld(sc_h[i], hTf, ['sc_h%d' % i], r=['hT'])
            else:
                if i == 0:
                    ld(hTf, sc_h[i], ['hT'], r=['sc_h%d' % i])
                    ld(la[:], sc_la[i], ['T0'], r=['sc_la%d' % i])
                if i + 1 < NT:
                    ld(hT2[1 - par][:].rearrange("p a b -> p (a b)"), sc_h[i + 1], ['hTp%d' % (1 - par)],
                       r=['sc_h%d' % (i + 1)])
                    ld(la2[1 - par][:], sc_la[i + 1], ['lap%d' % (1 - par)], r=['sc_la%d' % (i + 1)])
                ld(gqk[:, 256:512], sc_k[i][:, 0:256], ['gqk'], r=['sc_k%d' % i])
                ld(cs[:], cst[i], ['cs'])
                ld(rkb[:], sc_k[i][:, 256:768], ['rkb'], r=['sc_k%d' % i])
                ld(H.kendg[:].rearrange("p a b -> p (a b)"), sc_e[i][:, 0:512], ['kendg'], r=['sc_e%d' % i])
                ld(H.gv[:], sc_v[i][:, 0:512], ['gv'], r=['sc_v%d' % i])
                ld(H.kendr[:].rearrange("p a b -> p (a b)"), sc_e[i][:, 512:1536], ['kendr'], r=['sc_e%d' % i])
                ld(H.rv[:], sc_v[i][:, 512:1024], ['rv'], r=['sc_v%d' % i])
                ld(H.sbb1[:], sbs[i], ['sbb1'], r=['sbs%d' % i])
                ld(x_sb[:], xin[i * 128:(i + 1) * 128, :], ['x'])
            if lite:
                inproj(PA, hT, 1, C_GK, C_GK + 256)
                cp('act', gqk[:, 256:512], PA[1][:, 0:256], ['P1'], ['gqk'])
                ld(sc_k[i][:, 0:256], gqk[:, 256:512], ['sc_k%d' % i], r=['gqk'])
                inproj(PA, hT, 2, C_GV, C_GV + 512)
                cp('dve', H.gv[:], PA[2][:], ['P2'], ['gv'])
                ld(sc_v[i][:, 0:512], H.gv[:], ['sc_v%d' % i], r=['gv'])
            else:
                inproj(PA, hT, 1, C_GQ, C_GQ + 256)
                cp('act', gqk[:, 0:256], PA[1][:, 0:256], ['P1'], ['gqk'])
            if not lite:
                inproj(PA, hT, 1, C_GG, C_GG + 512)
                act(H.gg[:], PA[1][:], AF.Silu, ['P1'], ['gg'])
            if lite:
                fns = [mm(PA[2][0:32, 0:128], win[:, kc, C_LR:C_LR + 32], hT[:, kc, :], kc == 0) for kc in range(8)]
                pe(fns, r=['hT', 'win'], w=['P2'])
                cp('dve', lrT[0:32, :], PA[2][0:32, 0:128], ['P2'], ['lrT'])
                pe(mm(PA[1][:, :], lrT, wa[:, :], True), r=['lrT', 'wa'], w=['P1'])
                act(la[:], PA[1][:], AF.Exp, ['P1'], ['T0'], scale=-1.0)
                act(la[:], la[:], AF.Ln, ['T0'], ['T0'], bias=1.0)
                tsc('dve', la[:], la[:], -1.0 / 16.0, -1.0, ALU.mult, ALU.max, ['T0'], ['T0'])
                ld(sc_la[i], la[:], ['sc_la%d' % i], r=['T0'])
            if lite:
                pe([mm(PA[2][:, 0:256], tri[:, 2, :], la[:, 0:256], True),
                            mm(PA[2][:, 256:512], tri[:, 3, :], la[:, 256:512], False)], r=['tri', 'T0'], w=['P2'])
                ek = T[1]
                act(ek[:], PA[2][:], AF.Exp, ['P2'], ['T1'])
                tt('pool', H.kendg[:], b4(gqk[:, 256:512], 2), v3(ek[:], 2), ALU.mult, ['gqk', 'T1'], ['kendg'])
                ld(sc_e[i][:, 0:512], H.kendg[:].rearrange("p a b -> p (a b)"), ['sc_e%d' % i], r=['kendg'])
            fns = []
            for d in range(2):
                for p in range(2):
                    fns.append(mm(PA[1][:, d * 256 + p * 128:d * 256 + p * 128 + 128],
                                  la[:, d * 256 + p * 128:d * 256 + p * 128 + 128], tri[:, d, :],
                                  d == 0 and p == 0))
            pe(fns, r=['tri', 'T0'], w=['P1'])
            pf = PA[1][:, 0:256].rearrange("q (p j t) -> q p j t", p=2, j=2)[:, :, :, 63]
            pb = PA[1][:, 256:512].rearrange("q (p j t) -> q p j t", p=2, j=2)[:, :, :, 0]
            act(H.Dg[:, 0:4].rearrange("q (p j) -> q p j", p=2), pf, AF.Exp, ['P1'], ['Dg'])
            act(H.Dg[:, 4:8].rearrange("q (p j) -> q p j", p=2), pb, AF.Exp, ['P1'], ['Dg'])
            if not lite:
                act(T[2][:], PA[1][:], AF.Exp, ['P1'], ['T2'], bias=LN_QS)
                act(T[3][:], PA[1][:], AF.Exp, ['P1'], ['T3'], scale=-1.0)
                transposes(P0b, 0, gqk, 4, 0, ['gqk'])
                tt('dve', v3(H.qdT[:], 2), b4(P0b[:, 0:256], 2), v3(T[2][:], 2), ALU.mult, ['P0', 'T2'], ['qdT'])
                tt('dve', v3(H.kiT[:], 2), b4(P0b[:, 256:512], 2), v3(T[3][:], 2), ALU.mult, ['P0', 'T3'], ['kiT'])
            cosb = b4(cs[:, 0:64])
            sinb = b4(cs[:, 64:128])

            def rotary(bank, dst, dname, ta, tb):
                src = PA[bank][:].rearrange("p (h t f) -> p h t f", h=4, t=2)
                dv = dst[:].rearrange("p (h t f) -> p h t f", h=4, t=2)
                m1 = T[ta][:].rearrange("p (h t f) -> p h t f", h=4, t=2)
                m2 = T[tb][:].rearrange("p (h t f) -> p h t f", h=4, t=2)
                bk = 'P%d' % bank
                na, nb = 'T%d' % ta, 'T%d' % tb
                tt('dve', m1[:, :, 0, :], src[:, :, 0, :], cosb, ALU.mult, [bk, 'cs'], [na])
                tt('dve', m1[:, :, 1, :], src[:, :, 1, :], cosb, ALU.mult, [bk, 'cs'], [na])
                tt('dve', m2[:, :, 0, :], src[:, :, 1, :], sinb, ALU.mult, [bk, 'cs'], [nb])
                tt('dve', m2[:, :, 1, :], src[:, :, 0, :], sinb, ALU.mult, [bk, 'cs'], [nb])
                tt('pool', dv[:, :, 0, :], m1[:, :, 0, :], m2[:, :, 0, :], ALU.subtract, [na, nb], [dname])
                tt('pool', dv[:, :, 1, :], m1[:, :, 1, :], m2[:, :, 1, :], ALU.add, [na, nb], [dname])

            if lite:
                inproj(PA, hT, 2, C_RK, C_RK + 512)
                rotary(2, rkb, 'rkb', 0, 1)
                tt('pool', v3(H.kendr[:, 0, :], 4), v3(rkb[:], 4), rsc[:, 0:4].unsqueeze(2).to_broadcast([128, 4, 128]),
                   ALU.mult, ['rkb', 'rsc'], ['kendr'])
                tt('pool', v3(H.kendr[:, 1, :], 4), v3(rkb[:], 4), rsc[:, 4:8].unsqueeze(2).to_broadcast([128, 4, 128]),
                   ALU.mult, ['rkb', 'rsc'], ['kendr'])
                inproj(PA, hT, 1, C_RV, C_RV + 512)
                cp('act', H.rv[:], PA[1][:], ['P1'], ['rv'])
                ld(sc_k[i][:, 256:768], rkb[:], ['sc_k%d' % i], r=['rkb'])
                ld(sc_e[i][:, 512:1536], H.kendr[:].rearrange("p a b -> p (a b)"), ['sc_e%d' % i], r=['kendr'])
                ld(sc_v[i][:, 512:1024], H.rv[:], ['sc_v%d' % i], r=['rv'])
            if not lite:
                inproj(PA, hT, 2, C_RQ, C_RQ + 512)
                rotary(2, rqb, 'rqb', 2, 3)
                inproj(PA, hT, 1, C_RG, C_RG + 512)
                act(H.rg[:], PA[1][:], AF.Silu, ['P1'], ['rg'])
                transposes(P0b, 0, rqb, 4, 0, ['rqb'])
                transposes(P0b, 0, rkb, 4, 4, ['rkb'])
                tt('dve', qfbT[:, 0, :], P0b[:, 0:512], qfb[:, 0:512], ALU.mult, ['P0', 'qfb'], ['qfbT'])
                tt('dve', qfbT[:, 1, :], P0b[:, 0:512], qfb[:, 512:1024], ALU.mult, ['P0', 'qfb'], ['qfbT'])
                cp('act', H.rkT[:], P0b[:, 512:1024], ['P0'], ['rkT'])

        def kv_update(H, d, j, bg, br, state, sname, src=None, srcname=None):
            if src is None:
                src, srcname = state, sname
            rows = slice(64 * j, 64 * j + 64)
            fns = [mm(P[bg][:, 256 * p:256 * p + 256], H.kendg[rows, d, 128 * p:128 * p + 128],
                      H.gv[rows, 256 * p:256 * p + 256], p == 0) for p in range(2)]
            pe(fns, r=['kendg', 'gv'], w=['P%d' % bg])
            fns = [mm(P[br][:, 128 * h:128 * h + 128], H.kendr[rows, d, 128 * h:128 * h + 128],
                      H.rv[rows, 128 * h:128 * h + 128], h == 0) for h in range(4)]
            pe(fns, r=['kendr', 'rv'], w=['P%d' % br])
            for h in range(4):
                p, m = divmod(h, 2)
                rr = slice(64 * m, 64 * m + 64)
                cc = slice(256 * p + 128 * m, 256 * p + 128 * m + 128)
                dc = d * 4 + p * 2 + j
                stt(state[rr, cc], src[rr, cc], H.Dg[rr, dc:dc + 1], P[bg][rr, cc], ALU.mult, ALU.add,
                    [srcname, 'Dg', 'P%d' % bg], [sname])
            for h in range(4):
                cc = slice(512 + 128 * h, 512 + 128 * h + 128)
                stt(state[:, cc], src[:, cc], g64[:, h:h + 1], P[br][:, 128 * h:128 * h + 128],
                    ALU.mult, ALU.add, [srcname, 'g64', 'P%d' % br], [sname])

        def barrier(names):
            S.alias = {}
            S.op('pool', lambda e: e.memset(st[:, 0:1], 0.0), r=[], w=['st'] + list(names))

        def pass1_update(i, H):
            S.alias = H.alias
            if i % BT == BT - 1:
                kc = NT // BT + i // BT
                tsc('pool', Sst[:], Sst[:], keep[:, kc:kc + 1], None, ALU.mult, None, ['S', 'keep'], ['S'])
            cp('act', sfb0[:], Sst[:], ['S'], ['sfb0'])
            ld(sbs[i], sfb0[:], ['sbs%d' % i], r=['sfb0'])
            kv_update(H, 1, 1, 6, 7, Sst, 'S')
            kv_update(H, 1, 0, 6, 7, Sst, 'S')

        def pass1():
            barrier(['x1'] + V1NAMES + P1XNAMES)
            S.op('pool', lambda e: e.memset(V1.lrT, 1.0), w=['lrTv1'])
            wq = list(range(16))
            prev = None
            k = 0
            front_xload(NT - 1, HS[1])
            front_xload(NT - 2, HS[0])
            for i in reversed(range(0, NT, 2)):
                h1, h0 = (HS[1], HS[0]) if k % 2 == 0 else (HS[3], HS[2])
                k += 1
                S.begin()
                front(i + 1, True, h1, V1, xloaded=True, xnext=(i - 1 if i >= 2 else None))
                a1 = S.end()
                S.begin()
                front(i, True, h0, V0, xloaded=True, xnext=(i - 2 if i >= 2 else None))
                a0 = S.end()
                strs = [a1, a0]
                if prev is not None:
                    S.begin()
                    pass1_update(prev[0], prev[1])
                    pass1_update(prev[2], prev[3])
                    strs.insert(0, S.end())
                if wq:
                    S.begin()
                    wprep(wq.pop(0))
                    strs.append(S.end())
                S.play(strs, mode=_MODE)
                prev = (i + 1, h1, i, h0)
            pass1_update(prev[0], prev[1])
            pass1_update(prev[2], prev[3])
            while wq:
                wprep(wq.pop(0))
            barrier(['x1'] + V1NAMES + P1XNAMES)

        def stage_b(i, H):
            S.alias = H.alias
            t = i % 3
            x_sb = H.x_sb
            qdT, kiT, gv, rv = H.qdT, H.kiT, H.gv, H.rv
            if i % BT == 0:
                tsc('pool', Sst[:], Sst[:], keep[:, i // BT:i // BT + 1], None, ALU.mult, None, ['S', 'keep'], ['S'])
            tt('pool', v3(TB[1][:], 4), v3(H.gg[:], 4), b4(gnw[:]), ALU.mult, ['gg', 'gnw'], ['TB1'])
            tt('pool', TB[2][:], H.rg[:], rnw[:], ALU.mult, ['rg', 'rnw'], ['TB2'])
            tt('pool', TB[0][:], H.rg[:], rnb[:], ALU.mult, ['rg', 'rnb'], ['TB0'])
            cp('act', sfb0[:], Sst[:], ['S'], ['sfb0'])
            kv_update(H, 0, 0, 7, 3, Sst, 'S')
            cp('act', sfb1[:], Sst[:], ['S'], ['sfb1'])
            for m, bank in ((0, 4), (1, 5)):
                rr = slice(64 * m, 64 * m + 64)
                fns = []
                for d in range(2):
                    for p in range(2):
                        cc = slice(d * 256 + p * 128, d * 256 + p * 128 + 128)
                        fns.append(mm(P[bank][:, cc], kiT[rr, cc], qdT[rr, cc], d == 0 and p == 0))
                pe(fns, r=['kiT', 'qdT'], w=['P%d' % bank])
            t1 = mixed[:, 0:512]
            t2 = mixed[:, 512:1024]
            t1v = t1.rearrange("s (p m c) -> s p m c", p=2, m=2)
            t2v = t2.rearrange("s (p m c) -> s p m c", p=2, m=2)
            for m, bank in ((0, 4), (1, 5)):
                tt('dve', t1v[:, :, m, :], v3(P[bank][:, 0:256], 2), b4(tri[:, 0, :], 2), ALU.mult,
                   ['P%d' % bank, 'tri'], ['mxg'])
                tt('dve', t2v[:, :, m, :], v3(P[bank][:, 256:512], 2), b4(tri[:, 2, :], 2), ALU.mult,
                   ['P%d' % bank, 'tri'], ['mxr'])
            tt('dve', ATg[:], t1, t2, ALU.add, ['mxg', 'mxr'], ['ATg'])
            fns = [mm(P[6][:, 128 * h:128 * h + 128], H.rkT[:, 128 * h:128 * h + 128],
                      qfbT[:, 0, 128 * h:128 * h + 128], h == 0) for h in range(4)]
            pe(fns, r=['rkT', 'qfbT'], w=['P6'])
            tt('dve', ATr[:], P[6][:], mr[:], ALU.mult, ['P6', 'mr'], ['ATr'])
            kv_update(H, 1, 1, 7, 3, sbb0, 'sbb0', H.sbb1, 'sbb1')
            kv_update(H, 0, 1, 7, 3, Sst, 'S')
            fns = [mm(P[4][:, 128 * h:128 * h + 128], ATg[:, 128 * h:128 * h + 128], gv[:, 128 * h:128 * h + 128],
                      h == 0) for h in range(4)]
            for j, sfb, sbb in ((0, sfb0, sbb0), (1, sfb1, H.sbb1)):
                rows = slice(64 * j, 64 * j + 64)
                for p in range(2):
                    cc = slice(256 * p, 256 * p + 256)
                    c0 = 128 * p + 64 * j
                    fns.append(mm(P[4][rows, cc], qdT[:, c0:c0 + 64], sfb[:, cc], False))
                    fns.append(mm(P[4][rows, cc], qdT[:, 256 + c0:256 + c0 + 64], sbb[:, cc], False))
            pe(fns, r=['ATg', 'gv', 'qdT', 'sfb0', 'sfb1', 'sbb0', 'sbb1'], w=['P4'])
            fns = [mm(P[5][:, 128 * h:128 * h + 128], ATr[:, 128 * h:128 * h + 128], rv[:, 128 * h:128 * h + 128],
                      h == 0) for h in range(4)]
            for j, sfb, sbb in ((0, sfb0, sbb0), (1, sfb1, H.sbb1)):
                rows = slice(64 * j, 64 * j + 64)
                for h in range(4):
                    cc = slice(128 * h, 128 * h + 128)
                    sc = slice(512 + 128 * h, 512 + 128 * h + 128)
                    fns.append(mm(P[5][rows, cc], qfbT[:, 0, 128 * h + 64 * j:128 * h + 64 * j + 64], sfb[:, sc], False))
                    fns.append(mm(P[5][rows, cc], qfbT[:, 1, 128 * h + 64 * j:128 * h + 64 * j + 64], sbb[:, sc], False))
            pe(fns, r=['ATr', 'rv', 'qfbT', 'sfb0', 'sfb1', 'sbb0', 'sbb1'], w=['P5'])
            for h in range(4):
                cc = slice(128 * h, 128 * h + 128)
                act(hTB[:, h, :], P[4][:, cc], AF.Square, ['P4'], ['hTBg', 'st'], accum_out=st[:, h:h + 1])
            act(st[:, 0:4], st[:, 0:4], AF.Ln, ['st'], ['st'], scale=1.0 / 128, bias=EPS)
            act(st[:, 0:4], st[:, 0:4], AF.Exp, ['st'], ['st'], scale=-0.5)
            for h in range(4):
                cc = slice(128 * h, 128 * h + 128)
                stt(mixed[:, cc], P[4][:, cc], st[:, h:h + 1], TB[1][:, cc], ALU.mult, ALU.mult,
                    ['P4', 'st', 'TB1'], ['mxg'])
            S.op('dve', lambda e: e.reduce_sum(st[:, 4:8], v3(P[5][:], 4), axis=AX.X), r=['P5'], w=['st2'], cost=0.7)
            for h in range(4):
                cc = slice(128 * h, 128 * h + 128)
                act(hTB[:, 4 + h, :], P[5][:, cc], AF.Square, ['P5'], ['hTBr', 'st3'], accum_out=st[:, 8 + h:9 + h])
            tsc('dve', st[:, 4:8], st[:, 4:8], 1.0 / 128, None, ALU.mult, None, ['st2'], ['st2'])
            tt('dve', st[:, 12:16], st[:, 4:8], st[:, 4:8], ALU.mult, ['st2'], ['st2'])
            stt(st[:, 8:12], st[:, 8:12], 1.0 / 128, st[:, 12:16], ALU.mult, ALU.subtract, ['st2', 'st3'], ['st3'])
            for h in range(4):
                cc = slice(128 * h, 128 * h + 128)
                stt(TB[2][:, cc], P[5][:, cc], st[:, 4 + h:5 + h], TB[2][:, cc], ALU.subtract, ALU.mult,
                    ['P5', 'st2', 'TB2'], ['TB2'])
            act(st[:, 8:12], st[:, 8:12], AF.Ln, ['st3'], ['st3'], bias=EPS)
            act(st[:, 8:12], st[:, 8:12], AF.Exp, ['st3'], ['st3'], scale=-0.5)
            for h in range(4):
                cc = slice(128 * h, 128 * h + 128)
                stt(mixed[:, 512 + 128 * h:512 + 128 * h + 128], TB[2][:, cc], st[:, 8 + h:9 + h], TB[0][:, cc],
                    ALU.mult, ALU.add, ['TB2', 'st3', 'TB0'], ['mxr'])
            hTBf = hTB[:].rearrange("p a b -> p (a b)")
            for hf, mx, hn in ((0, 'mxg', 'hTBg'), (1, 'mxr', 'hTBr')):
                fns = [(lambda e, k=k: e.transpose(P3b[:, k * 128:(k + 1) * 128], mixed[:, k * 128:(k + 1) * 128], ident[:]))
                       for k in range(4 * hf, 4 * hf + 4)]
                pe(fns, r=[mx, 'ident'], w=['P3'])
                cp('act', hTBf[:, 512 * hf:512 * hf + 512], P3b[:, 512 * hf:512 * hf + 512], ['P3'], [hn])
                for n in range(2):
                    fns = [mm(P[6 + n][:, :], hTB[:, kc, :], wout[:, kc, 512 * n:512 * n + 512], kc == 0)
                           for kc in range(4 * hf, 4 * hf + 4)]
                    pe(fns, r=[hn, 'wout'], w=['P%d' % (6 + n)])
            for n in range(2):
                tt('dve', x1[:, t, 512 * n:512 * n + 512], P[6 + n][:, :], x_sb[:, 512 * n:512 * n + 512], ALU.add,
                   ['P%d' % (6 + n), 'x'], ['x1'])
            act(mixed[:], x1[:, t, :], AF.Square, ['x1'], ['mxg', 'mxr', 'ss1'], accum_out=ss[:, 1:2])
            rstd_from_ss(ss, 1, D)
            act(mixed[:], x1[:, t, :], AF.Copy, ['x1', 'ss1'], ['mxg', 'mxr'], scale=ss[:, 1:2])
            transposes(P3b, 3, mixed, 8, 0, ['mxg', 'mxr'])
            cp('dve', h2T[:, :, 128 * t:128 * t + 128], v3(P3b[:, 0:1024], 8), ['P3'], ['h2T'])

        def ffn(g, gs):
            S.alias = {}
            rtmp = TB[0]
            ntok = 128 * gs

            def ffn1(ft):
                pc = ft // 2
                if ft % 2 == 0:
                    ld(w1b[pc % 2][:].rearrange("p a b -> p (a b)"), w1s[pc], ['w1b%d' % (pc % 2)], r=['w1s%d' % pc])
                bank = 6 + (ft % 2)
                fns = [mm(P[bank][:, 0:ntok], w1b[pc % 2][:, kc, (ft % 2) * 128:(ft % 2) * 128 + 128],
                          h2T[:, kc, 0:ntok], kc == 0) for kc in range(8)]
                pe(fns, r=['w1b%d' % (pc % 2), 'h2T'], w=['P%d' % bank])
                act(rtmp[:, 0:ntok], P[bank][:, 0:ntok], AF.Relu, ['P%d' % bank], ['TB0'])
                tt('pool', ffT[:, ft % 4, 0:ntok], rtmp[:, 0:ntok], rtmp[:, 0:ntok], ALU.mult, ['TB0'], ['ffT%d' % (ft % 4)])

            def ffn2(ft):
                q = ft // 2
                if ft % 2 == 0:
                    ld(w2b[q % 2][:].rearrange("p a b -> p (a b)"), w2s[q], ['w2b%d' % (q % 2)], r=['w2s%d' % q])
                fns = []
                for t in range(gs):
                    for n in range(2):
                        fns.append(mm(P[2 * t + n][:, :], ffT[:, ft % 4, 128 * t:128 * t + 128],
                                      w2b[q % 2][:, ft % 2, 512 * n:512 * n + 512], ft == 0))
                pe(fns, r=['ffT%d' % (ft % 4), 'w2b%d' % (q % 2)], w=['P%d' % k for k in range(2 * gs)])

            ffn1(0)
            for ft in range(32):
                if ft + 1 < 32:
                    ffn1(ft + 1)
                ffn2(ft)
            for t in range(gs):
                for n in range(2):
                    sl = x1[:, t, 512 * n:512 * n + 512]
                    tt('dve', sl, P[2 * t + n][:, :], sl, ALU.add, ['P%d' % (2 * t + n), 'x1'], ['x1'])
            for t in range(gs):
                i = 3 * g + t
                act(mixed[:], x1[:, t, :], AF.Square, ['x1'], ['mxg', 'mxr', 'ss2'], accum_out=ss[:, 2:3])
                rstd_from_ss(ss, 2, D)
                stt(x1[:, t, :], x1[:, t, :], ss[:, 2:3], wnf[:], ALU.mult, ALU.mult, ['x1', 'ss2', 'wnf'], ['x1'])
                ld(yout[i * 128:(i + 1) * 128, :], x1[:, t, :], ['y%d' % i], r=['x1'])

        _MODE = 'model'
        S.marks = [max(S.m_eng.values())]
        pass1()
        S.marks.append(max(S.m_eng.values()))
        front(0, False, HS[0], V0)
        for i in range(NT):
            S.begin()
            stage_b(i, HS[i % 2])
            sb_ops = S.end()
            sa_ops = []
            if i + 1 < NT:
                S.begin()
                front(i + 1, False, HS[(i + 1) % 2], V0)
                sa_ops = S.end()
            S.play([s for s in (sb_ops, sa_ops) if s], mode=_MODE)
            if i % 3 == 2 or i == NT - 1:
                ffn(i // 3, i % 3 + 1)

        sems = {e: es.enter_context(nc.semaphore("sem_" + e)) for e in ENG if e != 'sp'}
        for de in ('sp', 'pool'):
            for j in range(S.nds):
                sems['d%s%d' % (de, j)] = es.enter_context(nc.semaphore("dsem_%s%d" % (de, j)))
        fin = S.final_tokens()
        block = es.enter_context(nc.Block())

        def emit(name, e):
            for waits, fns, (k, inc) in S.q[name]:
                for wk, wv in waits:
                    e.wait_ge(sems[wk], wv)
                for f in fns[:-1]:
                    f(e)
                fns[-1](e).then_inc(sems[k], inc)

        @block.tensor
        def _(e):
            emit('pe', e)

        @block.scalar
        def _(e):
            emit('act', e)

        @block.vector
        def _(e):
            emit('dve', e)

        @block.gpsimd
        def _(e):
            emit('pool', e)

        @block.sync
        def _(e):
            emit('sp', e)
            for k, v in fin:
                e.wait_ge(sems[k], v)
    return nc


RET_HEADS = 4
ROPE_BASE = 10000.0


def _const_tables():
    s = np.arange(128)[:, None]
    c = np.arange(128)[None, :]
    same = (s // 64) == (c // 64)
    tri = np.stack([same & (s <= c), same & (s >= c), same & (s > c), same & (s < c)]).astype(np.float32)
    lg = np.log1p(-np.power(2.0, -5.0 - np.arange(RET_HEADS, dtype=np.float64)))
    cc = (np.arange(128) % 64).astype(np.float64)
    mr = np.zeros((128, 4, 128), np.float64)
    qf = np.zeros((4, 128), np.float64)
    qb = np.zeros((4, 128), np.float64)
    rsc = np.zeros((128, 8), np.float64)
    for h in range(4):
        e = np.abs(c - s) - (cc[None, :] + 1.0)
        mr[:, h, :] = np.where(same, np.exp(lg[h] * e), 0.0)
        qf[h] = np.exp(lg[h] * (cc + 1.0)) * 128.0 ** -0.5
        qb[h] = np.exp(lg[h] * (64.0 - cc)) * 128.0 ** -0.5
        rsc[:, h] = np.exp(lg[h] * (63.0 - cc))
        rsc[:, 4 + h] = np.exp(lg[h] * cc)
    qfb = np.concatenate([np.broadcast_to(qf.reshape(1, 512), (128, 512)),
                          np.broadcast_to(qb.reshape(1, 512), (128, 512))], axis=1)
    g64 = np.broadcast_to(np.exp(lg * 64.0)[None, :], (128, 4))
    return dict(tri=tri, mr=mr.reshape(128, 512).astype(np.float32), qfb=np.ascontiguousarray(qfb, np.float32),
                rsc=rsc.astype(np.float32), g64=np.ascontiguousarray(g64, np.float32),
                ident=np.eye(128, dtype=np.float32))


def _rope_table(L):
    half = 64
    inv_freq = np.power(np.float32(ROPE_BASE), -np.arange(half, dtype=np.float32) / np.float32(half)).astype(np.float32)
    ang = np.arange(L, dtype=np.float32)[:, None] * inv_freq[None, :]
    return np.concatenate([np.cos(ang), np.sin(ang)], axis=1).astype(np.float32)


def _shared_inputs(inp):
    f = lambda a: np.ascontiguousarray(np.asarray(a, dtype=np.float32))
    wa = np.zeros((33, 512), np.float32)
    wa[0:16, 0:256] = f(inp['w_alpha_fwd'])[0]
    wa[16:32, 256:512] = f(inp['w_alpha_bwd'])[0]
    wa[32, 0:256] = f(inp['b_alpha_fwd'])[0]
    wa[32, 256:512] = f(inp['b_alpha_bwd'])[0]
    d = dict(
        w_in=f(inp['w_in'])[0], w_out=f(inp['w_out'])[0], w_ff1=f(inp['w_ff1'])[0], w_ff2=f(inp['w_ff2'])[0],
        wn1t=np.ascontiguousarray(f(inp['attn_norm_w'])[0].reshape(8, 128).T),
        wn2t=np.ascontiguousarray(f(inp['mlp_norm_w'])[0].reshape(8, 128).T),
        wnf_b=np.ascontiguousarray(np.broadcast_to(f(inp['final_norm_w'])[None, :], (128, D))),
        wa=wa,
        gnw_b=np.ascontiguousarray(np.broadcast_to(f(inp['gla_norm_w'])[0][None, :], (128, 128))),
        rnw_b=np.ascontiguousarray(np.broadcast_to(f(inp['ret_norm_w'])[0][None, :], (128, 512))),
        rnb_b=np.ascontiguousarray(np.broadcast_to(f(inp['ret_norm_b'])[0][None, :], (128, 512))),
    )
    d.update(_const_tables())
    return d


def _core_inputs(seqs, NT, shared, BT=16):
    x = np.zeros((NT * 128, D), np.float32)
    cst = np.zeros((NT, 128, 128), np.float32)
    NB = NT // BT
    keep = np.ones((128, 2 * NB), np.float32)
    pos = 0
    for s in seqs:
        L = s.shape[0]
        x[pos:pos + L] = s
        cst[pos // 128:(pos + L) // 128] = _rope_table(L).reshape(L // 128, 128, 128)
        assert (pos // 128) % BT == 0 and ((pos + L) // 128) % BT == 0
        keep[:, (pos // 128) // BT] = 0.0
        keep[:, NB + ((pos + L) // 128 - 1) // BT] = 0.0
        pos += L
    if pos < NT * 128:
        keep[:, (pos // 128) // BT] = 0.0
        keep[:, NB + NB - 1] = 0.0
    m = dict(shared)
    m.update(x=x, cst=cst, keep=keep)
    return m


_NC_CACHE = {}


def kernel(x_prompt, x_sample, **w):
    xp = np.asarray(x_prompt, dtype=np.float32)
    xs = np.asarray(x_sample, dtype=np.float32)
    NT, BT = 128, 16
    shared = _shared_inputs(w)
    plan = [[('s', 0)], [('s', 1)]]
    counts = [6, 6, 5, 5, 5, 5]
    b = 0
    for c in counts:
        plan.append([('p', b + k) for k in range(c)])
        b += c
    in_maps = []
    for core in plan:
        seqs = [xs[k] if kind == 's' else xp[k] for kind, k in core]
        in_maps.append(_core_inputs(seqs, NT, shared, BT))
    key = (NT, BT)
    if key not in _NC_CACHE:
        _NC_CACHE[key] = build_nc(NT, BT)
    res = run_bass_kernel_spmd(_NC_CACHE[key], in_maps, core_ids=list(range(8)))
    yp = np.zeros_like(xp)
    ys = np.zeros_like(xs)
    for ci, core in enumerate(plan):
        y = res.results[ci]["y"]
        pos = 0
        for kind, k in core:
            if kind == 's':
                ys[k] = y[pos:pos + xs.shape[1]]
                pos += xs.shape[1]
            else:
                yp[k] = y[pos:pos + xp.shape[1]]
                pos += xp.shape[1]
    return yp, ys
```
